# Optimizing a Trainium2 kernel written in Bass

```python
import jax, jax.numpy as jnp
from jax import lax
import numpy as np

D_MODEL = 1024
BATCH = 32
SEQ = 256
DEPTH = 1
DEC_BATCH = 4
DEC_SEQ = 2048
PAST_LEN = 256

GRID_W = 64
GLA_WIDTH = 512
GLA_HEADS = 4
GLA_DK = 64
GLA_DV = 128
QK_W = GLA_HEADS * GLA_DK
GLA_LOWRANK = 16
GLA_TAU = 16.0
GLA_CHUNK = 64
CONV_WIDTH = 512
CONV_K = 3
MIX_WIDTH = GLA_WIDTH + CONV_WIDTH
D_FF = 2816
N_MOD = 9
EPS = 1e-6
SPLITS = [QK_W, QK_W, GLA_WIDTH, GLA_WIDTH, GLA_LOWRANK, GLA_LOWRANK, CONV_WIDTH, CONV_WIDTH, CONV_WIDTH]
IN_COLS = sum(SPLITS)
SPLIT_IDX = list(np.cumsum(SPLITS)[:-1])

kernel_name = "hybrid_gla_shortconv_diffusion_step"


def rmsnorm(x, g):
    xf = x.astype(jnp.float32)
    y = xf * lax.rsqrt(jnp.mean(xf * xf, axis=-1, keepdims=True) + EPS)
    return (y * g.astype(jnp.float32)).astype(x.dtype)


def ada_mod(cvec, w, b):
    return (jax.nn.silu(cvec) @ w + b).reshape(cvec.shape[0], N_MOD, D_MODEL)


def swiglu(h, w1, w3, w2):
    return (jax.nn.silu(h @ w1) * (h @ w3)) @ w2


def conv3_centred(u, w):
    up = jnp.pad(u, [(0, 0)] * (u.ndim - 2) + [(1, 1), (0, 0)])
    return w[0] * up[..., :-2, :] + w[1] * up[..., 1:-1, :] + w[2] * up[..., 2:, :]


def gla_chunked(q, k, v, log_a, s0):
    b, h, l, _ = q.shape
    dv = v.shape[-1]
    n = l // GLA_CHUNK

    def to_chunks(t):
        return jnp.moveaxis(t.reshape(b, h, n, GLA_CHUNK, t.shape[-1]), 2, 0)

    mask = jnp.tril(jnp.ones((GLA_CHUNK, GLA_CHUNK), bool))[:, :, None]

    def step(s, inp):
        qc, kc, vc, ac = inp
        cum = jnp.cumsum(ac, axis=2)
        o_inter = jnp.einsum('bhtd,bhde->bhte', qc * jnp.exp(cum), s)
        diff = cum[:, :, :, None, :] - cum[:, :, None, :, :]
        decay = jnp.where(mask, jnp.exp(jnp.where(mask, diff, 0.0)), 0.0)
        scores = jnp.einsum('bhtd,bhsd,bhtsd->bhts', qc, kc, decay)
        o_intra = jnp.einsum('bhts,bhse->bhte', scores, vc)
        total = cum[:, :, -1:, :]
        s_new = jnp.exp(total[:, :, 0, :])[..., None] * s + jnp.einsum(
            'bhsd,bhse->bhde', kc * jnp.exp(total - cum), vc)
        return s_new, o_inter + o_intra

    s_fin, o = lax.scan(step, s0, (to_chunks(q), to_chunks(k), to_chunks(v), to_chunks(log_a)))
    return jnp.moveaxis(o, 0, 2).reshape(b, h, l, dv), s_fin


def heads(t, dh):
    b, l, _ = t.shape
    return jnp.transpose(t.reshape(b, l, GLA_HEADS, dh), (0, 2, 1, 3)).astype(jnp.float32)


def mixer(h, w_in, w_decay, b_decay, gla_norm, conv_w, w_out, s0_f, s0_b, rows):
    b, l, _ = h.shape
    proj = h @ w_in
    q, k, v, g, lr_f, lr_b, cb, cc, ch = jnp.split(proj, SPLIT_IDX, axis=-1)
    qh = heads(q, GLA_DK) * (GLA_DK ** -0.5)
    kh = heads(k, GLA_DK)
    vh = heads(v, GLA_DV)
    la_f = heads(jax.nn.log_sigmoid((lr_f @ w_decay[0] + b_decay[0]).astype(jnp.float32)) / GLA_TAU, GLA_DK)
    la_b = heads(jax.nn.log_sigmoid((lr_b @ w_decay[1] + b_decay[1]).astype(jnp.float32)) / GLA_TAU, GLA_DK)
    o_f, s_f = gla_chunked(qh, kh, vh, la_f, s0_f.astype(jnp.float32))
    fl = lambda t: jnp.flip(t, axis=2)
    o_b, s_b = gla_chunked(fl(qh), fl(kh), fl(vh), fl(la_b), s0_b.astype(jnp.float32))
    o = o_f + fl(o_b)
    o = o * lax.rsqrt(jnp.mean(o * o, axis=-1, keepdims=True) + EPS)
    o = jnp.transpose(o, (0, 2, 1, 3)).reshape(b, l, GLA_WIDTH) * gla_norm.astype(jnp.float32)
    o_gla = (o * jax.nn.silu(g.astype(jnp.float32))).astype(h.dtype)
    u = cc * ch
    if rows is None:
        cu = conv3_centred(u, conv_w)
    else:
        cu = conv3_centred(u.reshape(b, rows, GRID_W, CONV_WIDTH), conv_w).reshape(b, l, CONV_WIDTH)
    o_conv = cb * cu
    return jnp.concatenate([o_gla, o_conv], axis=-1) @ w_out, s_f, s_b


def block(x, mods, s0_f, s0_b, rows, norm_ffn1, w1_ffn1, w3_ffn1, w2_ffn1, norm_mix, w_in,
          w_decay, b_decay, gla_norm, conv_w, w_out, norm_ffn2, w1_ffn2, w3_ffn2, w2_ffn2):
    m = lambda i: mods[:, i, None, :]
    h = rmsnorm(x, norm_ffn1) * (1.0 + m(1)) + m(0)
    x = x + 0.5 * m(2) * swiglu(h, w1_ffn1, w3_ffn1, w2_ffn1)
    h = rmsnorm(x, norm_mix) * (1.0 + m(4)) + m(3)
    mix, s_f, s_b = mixer(h, w_in, w_decay, b_decay, gla_norm, conv_w, w_out, s0_f, s0_b, rows)
    x = x + m(5) * mix
    h = rmsnorm(x, norm_ffn2) * (1.0 + m(7)) + m(6)
    x = x + 0.5 * m(8) * swiglu(h, w1_ffn2, w3_ffn2, w2_ffn2)
    return x, s_f, s_b


def setup_inputs(seed: int = 0) -> dict:
    key = jax.random.key(seed)
    ks = jax.random.split(key, 32)
    nrm = lambda i, shape, s: jax.random.normal(ks[i], shape, jnp.float32) * s
    gain = lambda i, shape: 1.0 + 0.05 * jax.random.normal(ks[i], shape, jnp.float32)
    return {
        "x_prompt": nrm(0, (BATCH, SEQ, D_MODEL), 1.0),
        "x_sample": nrm(1, (DEC_BATCH, DEC_SEQ, D_MODEL), 1.0),
        "state_gla": nrm(2, (DEC_BATCH, DEPTH, 2, GLA_HEADS, GLA_DK, GLA_DV), 0.5),
        "c": nrm(3, (DEC_BATCH, D_MODEL), 1.0),
        "c_ctx": nrm(4, (D_MODEL,), 1.0),
        "w_ada": nrm(5, (DEPTH, D_MODEL, N_MOD * D_MODEL), D_MODEL ** -0.5),
        "b_ada": nrm(6, (DEPTH, N_MOD * D_MODEL), 0.02),
        "norm_ffn1": gain(7, (DEPTH, D_MODEL)),
        "w1_ffn1": nrm(8, (DEPTH, D_MODEL, D_FF), D_MODEL ** -0.5),
        "w3_ffn1": nrm(9, (DEPTH, D_MODEL, D_FF), D_MODEL ** -0.5),
        "w2_ffn1": nrm(10, (DEPTH, D_FF, D_MODEL), D_FF ** -0.5),
        "norm_mix": gain(11, (DEPTH, D_MODEL)),
        "w_in": nrm(12, (DEPTH, D_MODEL, IN_COLS), D_MODEL ** -0.5),
        "w_decay": nrm(13, (DEPTH, 2, GLA_LOWRANK, QK_W), GLA_LOWRANK ** -0.5),
        "b_decay": nrm(14, (DEPTH, 2, QK_W), 0.1),
        "gla_norm": gain(15, (DEPTH, GLA_WIDTH)),
        "conv_w": nrm(16, (DEPTH, CONV_K, CONV_WIDTH), CONV_K ** -0.5),
        "w_out": nrm(17, (DEPTH, MIX_WIDTH, D_MODEL), MIX_WIDTH ** -0.5),
        "norm_ffn2": gain(18, (DEPTH, D_MODEL)),
        "w1_ffn2": nrm(19, (DEPTH, D_MODEL, D_FF), D_MODEL ** -0.5),
        "w3_ffn2": nrm(20, (DEPTH, D_MODEL, D_FF), D_MODEL ** -0.5),
        "w2_ffn2": nrm(21, (DEPTH, D_FF, D_MODEL), D_FF ** -0.5),
        "final_norm": gain(22, (D_MODEL,)),
    }


def reference(x_prompt, x_sample, state_gla, c, c_ctx, w_ada, b_ada, norm_ffn1, w1_ffn1, w3_ffn1,
              w2_ffn1, norm_mix, w_in, w_decay, b_decay, gla_norm, conv_w, w_out, norm_ffn2,
              w1_ffn2, w3_ffn2, w2_ffn2, final_norm):
    rows = x_sample.shape[1] // GRID_W
    xp, xs = x_prompt, x_sample
    new_states = []
    for l in range(DEPTH):
        lw = (norm_ffn1[l], w1_ffn1[l], w3_ffn1[l], w2_ffn1[l], norm_mix[l], w_in[l], w_decay[l],
              b_decay[l], gla_norm[l], conv_w[l], w_out[l], norm_ffn2[l], w1_ffn2[l], w3_ffn2[l], w2_ffn2[l])
        mod_ctx = ada_mod(c_ctx[None, :], w_ada[l], b_ada[l])
        zeros = jnp.zeros((xp.shape[0], GLA_HEADS, GLA_DK, GLA_DV), jnp.float32)
        xp, s_f, s_b = block(xp, mod_ctx, zeros, zeros, None, *lw)
        new_states.append(jnp.stack([s_f, s_b], axis=1))
        mod_lat = ada_mod(c, w_ada[l], b_ada[l])
        xs, _, _ = block(xs, mod_lat, state_gla[:, l, 0], state_gla[:, l, 1], rows, *lw)
    y_prompt = rmsnorm(xp, final_norm)
    y_sample = rmsnorm(xs, final_norm)
    new_state_gla = jnp.stack(new_states, axis=1)
    return (y_prompt, y_sample, new_state_gla)
```

```python
import contextlib
import numpy as np
import concourse.bass as bass
import concourse.mybir as mybir
from concourse.bass_utils import run_bass_kernel_spmd

F32 = mybir.dt.float32
BF16 = mybir.dt.bfloat16
AF = mybir.ActivationFunctionType
ALU = mybir.AluOpType

ENGS = ("pe", "act", "dve", "pool", "sp")
T = 2048
NB = 4
D = 1024
DFF = 2816
NFC = 22
EPS = 1e-6
GROUPS = [(0, 8), (8, 16), (16, 22)]


class Tile:
    __slots__ = ("name", "writer", "readers", "excl")

    def __init__(self, name):
        self.name = name
        self.writer = None
        self.readers = {}
        self.excl = False


class Op:
    __slots__ = ("eng", "fn", "deps", "dma", "sem", "val", "signal", "idx", "phase")

    def __init__(self, eng, fn, dma, idx):
        self.eng = eng
        self.fn = fn
        self.deps = []
        self.dma = dma
        self.sem = None
        self.val = 0
        self.signal = False
        self.idx = idx


class Prog:
    def __init__(self):
        self.ops = []
        self.dma_counts = {}
        self.phase = "init"

    def tiles(self, name, *dims):
        if len(dims) == 0:
            return Tile(name)
        return [self.tiles(f"{name}_{i}", *dims[1:]) for i in range(dims[0])]

    def add(self, eng, fn, reads=(), writes=(), dma_key=None):
        op = Op(eng, fn, dma_key is not None, len(self.ops))
        op.phase = self.phase
        deps = {}
        for t in reads:
            if t.writer is not None:
                deps[id(t.writer)] = t.writer
            if t.excl:
                for k, r in t.readers.items():
                    if k != eng:
                        deps[id(r)] = r
        for t in writes:
            if t.writer is not None:
                deps[id(t.writer)] = t.writer
            for r in t.readers.values():
                deps[id(r)] = r
        op.deps = list(deps.values())
        for t in reads:
            if op.dma:
                t.readers[("dma", len(t.readers))] = op
            else:
                t.readers[eng] = op
        for t in writes:
            t.writer = op
            t.readers = {}
        if dma_key is not None:
            c = self.dma_counts.get(dma_key, 0) + 1
            self.dma_counts[dma_key] = c
            op.sem = ("dma", dma_key)
            op.val = 16 * c
            op.signal = True
        self.ops.append(op)
        return op

    def retire(self, tiles):
        tok = {}
        for t in tiles:
            cands = list(t.readers.values())
            if t.writer is not None:
                cands.append(t.writer)
            for o in cands:
                if o.dma:
                    tok[("dma", id(o))] = o
                else:
                    kk = ("c", o.eng)
                    if kk not in tok or tok[kk].idx < o.idx:
                        tok[kk] = o
        return tok

    def inherit(self, tiles, tok):
        for t in tiles:
            t.writer = None
            t.readers = dict(tok)

    def finalize(self):
        for op in self.ops:
            for d in op.deps:
                if not d.dma:
                    if d.eng == "pe" and op.eng == "pe":
                        continue
                    d.signal = True
        counts = {e: 0 for e in ENGS}
        for op in self.ops:
            if not op.dma and op.signal:
                counts[op.eng] += 1
                op.sem = ("eng", op.eng)
                op.val = counts[op.eng]


def emit_engine(prog, eng, handle, sems):
    waited = {}
    for op in prog.ops:
        if op.eng != eng:
            continue
        need = {}
        for d in op.deps:
            if d.sem is None:
                continue
            if d.eng == "pe" and eng == "pe" and not d.dma:
                continue
            if d.val > need.get(d.sem, 0):
                need[d.sem] = d.val
        for k, v in need.items():
            if v > waited.get(k, 0):
                handle.wait_ge(sems[k], v)
                waited[k] = v
        ins = op.fn(handle)
        if op.signal:
            ins.then_inc(sems[op.sem], 16 if op.dma else 1)


class StopBuild(Exception):
    pass


def flat(x):
    out = []
    if isinstance(x, (list, tuple)):
        for y in x:
            out.extend(flat(y))
    else:
        out.append(x)
    return out


def build_program(debug=None):
    nc = bass.Bass("TRN2", target_bir_lowering=False)
    P = Prog()
    es = contextlib.ExitStack()

    def dram(name, shape, dt=F32, out=False):
        return nc.dram_tensor(name, shape, dt, kind="ExternalOutput" if out else "ExternalInput").ap()

    x_d = dram("x", [T, D])
    vecA_d = dram("vecA", [80, 128])
    vecB_d = dram("vecB", [52, 128])
    s0_d = dram("s0", [2, 4, 64, 128])
    flags_d = dram("flags", [128, 2])
    cst_d = dram("cst", [128, 384])
    wada_d = dram("w_ada", [D, 9 * D])
    w1_d = [dram("w1_ffn1", [D, DFF]), dram("w1_ffn2", [D, DFF])]
    w3_d = [dram("w3_ffn1", [D, DFF]), dram("w3_ffn2", [D, DFF])]
    w2_d = [dram("w2_ffn1", [DFF, D]), dram("w2_ffn2", [DFF, D])]
    win_d = dram("w_in", [D, 3104])
    wdec_d = dram("w_decay", [2, 16, 256])
    wout_d = dram("w_out", [D, D])
    y_d = dram("y", [T, D], out=True)
    st_d = dram("st", [8, 2, 4, 64, 128], out=True)
    dbg_d = dram("dbg", [8, 128, T], out=True) if debug else None

    with es:
        def sb(name, shape, dt):
            return es.enter_context(nc.sbuf_tensor("sb_" + name, shape, dt))

        xT = sb("xT", [128, 8, T], F32)
        hT = sb("hT", [128, 8, T], BF16)
        NA = 46080
        arena = sb("arena", [128, NA], BF16)
        sqr = sb("sqr", [128, 2, 512], BF16)
        tmpr = sb("tmpr", [128, 2, 512], F32)
        lnv2 = sb("lnv", [128, 2, 512], F32)
        rstd2 = sb("rstd", [128, 2, 512], F32)
        cst = sb("cst", [128, 384], F32)
        identB = sb("identB", [128, 128], BF16)
        onesB = sb("onesB", [128, 128], BF16)
        stA = sb("stA", [128, 128], F32)
        stB = sb("stB", [128, 128], F32)
        vA = sb("vA", [128, 80], F32)
        vB = sb("vB", [128, 52], F32)
        modT = sb("modT", [128, 72], F32)
        der = sb("der", [128, 64], F32)
        flg = sb("flg", [128, 2], F32)
        scB = sb("scB", [128, 8], BF16)
        s0sb = sb("s0sb", [128, 4, 128], F32)
        Wd = sb("Wd", [32, 4, 128], BF16)
        eaT = sb("eaT", [128, 2, 16], F32)
        ps = es.enter_context(nc.psum_tensor("ps", [128, 8, 512], F32))

        identF = cst[:, 0:128]
        maskFB = cst[:, 128:384]

        def av_bf(off, n):
            return arena[:, off // 2: off // 2 + n]

        def av_f32(off, n):
            return arena[:, off // 2: off // 2 + 2 * n].bitcast(F32)

        sems = {}

        def getsem(k):
            if k not in sems:
                sems[k] = es.enter_context(nc.semaphore("s%d" % len(sems)))
            return sems[k]

        for e in ENGS:
            getsem(("eng", e))

        t_xT = P.tiles("xT", 8, NB)
        t_hT = P.tiles("hT", 8, NB)
        t_ps = P.tiles("ps", 8)
        for t in t_ps:
            t.excl = True
        t_sq = P.tiles("sq", 2)
        t_tmp = P.tiles("tmp", 2)
        t_lnv2 = P.tiles("lnv", 2)
        t_rstd2 = P.tiles("rstd", 2)
        t_cst = P.tiles("cst")
        t_idB = P.tiles("idB")
        t_ones = P.tiles("ones")
        t_stA = P.tiles("stA")
        t_stB = P.tiles("stB")
        t_vA = P.tiles("vA")
        t_vB = P.tiles("vB")
        t_modi = P.tiles("mod", 9)
        t_mod = t_modi
        t_derA = P.tiles("derA")
        t_derB = P.tiles("derB")
        t_derC = P.tiles("derC")
        t_derD = P.tiles("derD")
        t_der = [t_derA, t_derB, t_derC, t_derD]
        t_flg = P.tiles("flg")
        t_sc = P.tiles("sc")
        t_s0 = P.tiles("s0")
        t_Wd4 = P.tiles("Wd", 4)
        t_Wd = t_Wd4
        t_ea = P.tiles("ea")
        t_y = P.tiles("y")
        t_st = P.tiles("st")

        def bank(i):
            return t_ps[i]

        def dma(q, out, in_, reads, writes, key):
            getsem(("dma", key))
            return P.add(q, lambda e: e.dma_start(out=out, in_=in_), reads=flat(reads), writes=flat(writes), dma_key=key)

        def mm(out, lhsT, rhs, start, stop, reads, writes):
            return P.add("pe", lambda e: e.matmul(out, lhsT, rhs, start=start, stop=stop),
                         reads=flat(reads), writes=flat(writes))

        def tr(out, in_, ident, reads, writes):
            return P.add("pe", lambda e: e.transpose(out=out, in_=in_, identity=ident),
                         reads=flat(reads), writes=flat(writes))

        def act(out, in_, func, reads, writes, scale=None, bias=None):
            kw = {}
            if scale is not None:
                kw["scale"] = scale
            if bias is not None:
                kw["bias"] = bias
            return P.add("act", lambda e: e.activation(out=out, in_=in_, func=func, **kw),
                         reads=flat(reads), writes=flat(writes))

        def tt(eng, out, in0, in1, op, reads, writes):
            return P.add(eng, lambda e: e.tensor_tensor(out=out, in0=in0, in1=in1, op=op),
                         reads=flat(reads), writes=flat(writes))

        def ts(eng, out, in0, s1, s2, op0, op1, reads, writes):
            if s2 is None:
                return P.add(eng, lambda e: e.tensor_scalar(out=out, in0=in0, scalar1=s1, scalar2=None, op0=op0),
                             reads=flat(reads), writes=flat(writes))
            return P.add(eng, lambda e: e.tensor_scalar(out=out, in0=in0, scalar1=s1, scalar2=s2, op0=op0, op1=op1),
                         reads=flat(reads), writes=flat(writes))

        def stt(out, in0, scalar, in1, op0, op1, reads, writes):
            return P.add("dve", lambda e: e.scalar_tensor_tensor(out=out, in0=in0, scalar=scalar, in1=in1, op0=op0, op1=op1),
                         reads=flat(reads), writes=flat(writes))

        def cp(eng, out, in_, reads, writes):
            if eng == "act":
                return act(out, in_, AF.Copy, reads, writes)
            return P.add(eng, lambda e: e.tensor_copy(out=out, in_=in_), reads=flat(reads), writes=flat(writes))

        def memset(eng, out, val, writes):
            return P.add(eng, lambda e: e.memset(out, val), writes=flat(writes))

        cnt = {"evac": 0}

        def alt_eng():
            cnt["evac"] += 1
            return "act" if cnt["evac"] % 2 else "dve"

        OFF_GT = 0
        OFF_W13 = OFF_GT + 8 * 4096
        OFF_W2 = OFF_W13 + 6 * 4096
        OFF_SIL = OFF_W2 + 8 * 2048
        assert OFF_SIL + 2 * 2048 <= NA * 2
        OFF_XS = 0
        OFF_YT = 0
        OFF_YS = 8 * 2048
        OFF_LR = 0
        OFF_RM = OFF_LR + 4096
        OFF_QK = OFF_RM + 4096
        OFF_MIX = OFF_QK + 16384
        OFF_WS = OFF_MIX + 16384
        OFF_U = OFF_WS + 3 * 4096
        OFF_L = OFF_U
        OFF_C = OFF_L + 16384
        OFF_EX = OFF_C + 16384
        OFF_KT = OFF_U
        OFF_VT = OFF_KT + 8192
        OFF_SG = OFF_VT + 8192
        OFF_XP = OFF_SG + 8192
        OFF_AT = OFF_XP + 8192
        OFF_X = OFF_AT + 2048
        assert OFF_X + 4096 <= OFF_U + 38912 and OFF_EX + 6144 <= OFF_U + 38912
        assert OFF_U + 38912 <= NA * 2
        OFF_CV = OFF_U

        arena_tiles = []

        def remap(new_tiles):
            tok = P.retire(arena_tiles)
            P.inherit(flat(new_tiles), tok)
            arena_tiles.clear()
            arena_tiles.extend(flat(new_tiles))

        dma("sp", cst[:], cst_d, [], [t_cst], "c0")
        dma("sp", stA[0:80, :], vecA_d, [], [t_stA], "c1")
        dma("sp", stB[0:52, :], vecB_d, [], [t_stB], "c2")
        dma("sp", flg[:], flags_d, [], [t_flg], "c3")
        dma("sp", s0sb[:], s0_d.rearrange("r (a h) d e -> (h d) (r a) e", h=2), [], [t_s0], "c4")
        memset("pool", Wd[:], 0.0, t_Wd4)
        memset("dve", onesB[:], 1.0, [t_ones])
        cp("dve", identB[:], identF, [t_cst], [t_idB])

        t_xs = P.tiles("xs", 4)
        remap([t_xs])
        XS_OFF = (0, 16384, 57344, 73728)
        xs = [av_f32(XS_OFF[i], 4096).rearrange("p (j f) -> p j f", f=1024) for i in range(4)]

        def load_x(blk):
            dma("sp", xs[blk], x_d[blk * 512:(blk + 1) * 512, :].rearrange("(j p) f -> p j f", p=128),
                [], [t_xs[blk]], "xs%d" % blk)

        for blk_ in range(4):
            load_x(blk_)

        tr(ps[:, 7, 0:80], stA[0:80, :], identF[0:80, 0:80], [t_stA, t_cst], [bank(7)])
        tr(ps[:, 5, 0:52], stB[0:52, :], identF[0:52, 0:52], [t_stB, t_cst], [bank(5)])
        cp("dve", vA[:], ps[:, 7, 0:80], [bank(7)], [t_vA])
        cp("dve", vB[:], ps[:, 5, 0:52], [bank(5)], [t_vB])
        act(scB[:], vA[:, 72:80], AF.Silu, [t_vA], [t_sc])

        N1, NM, N2, FN, GLN, CW, BD = 0, 8, 16, 24, 32, 36, 48
        GM1, GMM, GM2, G1, G2, NBD, W0F, W2F = 0, 8, 16, 24, 32, 40, 44, 48

        def in_transposes(blk):
            for kc in range(8):
                b = kc % 4
                for j in range(4):
                    tr(ps[:, b, j * 128:(j + 1) * 128], xs[blk][:, j, kc * 128:(kc + 1) * 128], identF,
                       [t_xs[blk], t_cst], [bank(b)])
                cp(alt_eng(), xT[:, kc, blk * 512:(blk + 1) * 512], ps[:, b, :], [bank(b)], [t_xT[kc][blk]])

        def wslot_view(off, i):
            return av_bf(off + i * 4096, 2048).rearrange("p (k f) -> p k f", f=256)

        t_w13 = P.tiles("w13", 6)
        t_w2 = P.tiles("w2s", 8)
        t_gT = P.tiles("gT", 8, NB)
        t_sil = P.tiles("sil", 2)
        ffn_tiles = [t_w13, t_w2, t_gT, t_sil]

        wada_v = wada_d.rearrange("(k p) f -> p k f", p=128)

        mods_state = {"bank": 0}
        ring = {"n": 0}

        def ring_alloc2():
            n = ring["n"]
            ring["n"] += 2
            return n % 6, (n + 1) % 6

        def mjob_issue(j0):
            sl = ring_alloc2()
            views = []
            for i, s_ in enumerate(sl):
                wv = wslot_view(OFF_W13, s_)
                c0 = (j0 // 2 + i) * 256
                dma("pool", wv, wada_v[:, :, c0:c0 + 256], [], [t_w13[s_]], "ws%d" % s_)
                views.append((wv, s_))
            return views

        pending = []

        def flush_pending():
            while pending:
                pending.pop(0)()

        def mjob_run(j0, views):
            k = mods_state["bank"]
            mods_state["bank"] += 1
            rs = k % 2
            for i, (wv, s_) in enumerate(views):
                for kc in range(8):
                    mm(ps[0:1, 6, i * 256:(i + 1) * 256], scB[:, kc:kc + 1], wv[:, kc, :], kc == 0, kc == 7,
                       [t_w13[s_], t_sc], [bank(6)])
            act(lnv2[0:1, rs, :], ps[0:1, 6, :], AF.Copy, [bank(6)], [t_lnv2[rs]])

            def k1():
                for jj in range(4):
                    j = j0 + jj
                    mm(ps[:, 7, j:j + 1], lnv2[0:1, rs, jj * 128:(jj + 1) * 128], identF[0:1, 0:1], True, True,
                       [t_lnv2[rs], t_cst], [bank(7)])
                tt("dve", modT[:, j0:j0 + 4], ps[:, 7, j0:j0 + 4], vA[:, j0:j0 + 4], ALU.add, [bank(7), t_vA],
                   [t_modi[j0 // 8]])
            pending.append(k1)

        def run_jobs(jobs):
            loads = [j for j in jobs if j[0] == "load"]
            issued = {}
            st = {"n": 0}

            def ensure(upto):
                while st["n"] < min(upto, len(loads)):
                    issued[id(loads[st["n"]])] = loads[st["n"]][1]()
                    st["n"] += 1

            li = 0
            for j in jobs:
                if j[0] == "load":
                    ensure(li + 3)
                    ld = issued.pop(id(j))
                    flush_pending()
                    j[2](ld)
                    li += 1
                elif j[0] == "prefetch":
                    ensure(li + j[1])
                else:
                    flush_pending()
                    j[1]()
            flush_pending()

        def mjob(j0):
            return ("load", lambda: mjob_issue(j0), lambda v: mjob_run(j0, v))

        def mod(i):
            return modT[:, i * 8:(i + 1) * 8]

        def derive_a():
            stt(der[:, GM1:GM1 + 8], mod(1), 1.0, vB[:, N1:N1 + 8], ALU.add, ALU.mult, [t_modi[1], t_vB], [t_derA])

        def derive_b():
            ts("dve", der[:, G1:G1 + 8], mod(2), 0.5, None, ALU.mult, None, [t_modi[2]], [t_derB])

        def derive_c():
            rd = [t_modi[4], t_vB, t_flg]
            stt(der[:, GMM:GMM + 8], mod(4), 1.0, vB[:, NM:NM + 8], ALU.add, ALU.mult, rd, [t_derC])
            ts("dve", der[:, NBD:NBD + 4], vB[:, BD:BD + 4], -1.0, None, ALU.mult, None, rd, [t_derC])
            ts("dve", der[:, W0F:W0F + 4], vB[:, CW:CW + 4], flg[:, 1:2], None, ALU.mult, None, rd, [t_derC])
            ts("dve", der[:, W2F:W2F + 4], vB[:, CW + 8:CW + 12], flg[:, 1:2], None, ALU.mult, None, rd, [t_derC])

        def derive_d():
            rd = [t_modi[7], t_modi[8], t_vB]
            stt(der[:, GM2:GM2 + 8], mod(7), 1.0, vB[:, N2:N2 + 8], ALU.add, ALU.mult, rd, [t_derD])
            ts("dve", der[:, G2:G2 + 8], mod(8), 0.5, None, ALU.mult, None, rd, [t_derD])

        def norm_stats(blk, sl=0):
            for kc in range(8):
                s = kc % 2
                act(sqr[:, s, :], xT[:, kc, blk * 512:(blk + 1) * 512], AF.Square, [t_xT[kc][blk]], [t_sq[s]])
                mm(ps[:, 7, :], onesB[:], sqr[:, s, :], kc == 0, kc == 7, [t_ones, t_sq[s]], [bank(7)])
            act(lnv2[:, sl, :], ps[:, 7, :], AF.Ln, [bank(7)], [t_lnv2[sl]], scale=1.0 / D, bias=EPS)
            act(rstd2[:, sl, :], lnv2[:, sl, :], AF.Exp, [t_lnv2[sl]], [t_rstd2[sl]], scale=-0.5)

        def norm_apply(blk, gm_col, sh_mod, t_gm, act_heavy=False, tmps=None, kcs=None):
            sl = blk % 2
            if tmps is None:
                tmps = [(tmpr[:, 0, :], t_tmp[0]), (tmpr[:, 1, :], t_tmp[1])]
            for kc in (range(8) if kcs is None else kcs):
                tv, ttile = tmps[kc % len(tmps)]
                xv = xT[:, kc, blk * 512:(blk + 1) * 512]
                hv = hT[:, kc, blk * 512:(blk + 1) * 512]
                tt("dve", tv, xv, rstd2[:, sl, :], ALU.mult, [t_xT[kc][blk], t_rstd2[sl]], [ttile])
                gm = der[:, gm_col + kc:gm_col + kc + 1]
                sh = modT[:, sh_mod * 8 + kc:sh_mod * 8 + kc + 1]
                rd = [ttile, t_gm, t_modi[sh_mod]]
                if kc % 2 == 1:
                    ts("pool", hv, tv, gm, sh, ALU.mult, ALU.add, rd, [t_hT[kc][blk]])
                elif kc % 4 == 0 or act_heavy:
                    act(hv, tv, AF.Identity, rd, [t_hT[kc][blk]], scale=gm, bias=sh)
                else:
                    ts("dve", hv, tv, gm, sh, ALU.mult, ALU.add, rd, [t_hT[kc][blk]])

        def norm_mod(gm_col, sh_mod, t_gm, tmps=None):
            P.phase = "norm%d" % sh_mod
            norm_stats(0, 0)
            for blk in range(NB):
                if blk + 1 < NB:
                    norm_stats(blk + 1, (blk + 1) % 2)
                norm_apply(blk, gm_col, sh_mod, t_gm, tmps=tmps)

        def tmps4():
            sil_v = [av_f32(OFF_SIL + i * 2048, 512) for i in range(2)]
            return [(tmpr[:, 0, :], t_tmp[0]), (sil_v[0], t_sil[0]), (tmpr[:, 1, :], t_tmp[1]), (sil_v[1], t_sil[1])]

        def make_norm_tail(apply_fn, label, begin_extra=None, sq_off=None, alias_fn=None, lag=1, slice_fn=None, post_fn=None, sq_offs=None, defer=None):
            st = {}

            def begin():
                P.phase = "tail_" + label
                if begin_extra is not None:
                    begin_extra()
                t_sq8 = P.tiles("sq8", 2, 8)
                if alias_fn is not None:
                    alias_fn(t_sq8)
                else:
                    P.inherit(flat(t_sq8), P.retire(flat(t_w13)))
                    arena_tiles.extend(flat(t_sq8))
                so = OFF_W13 if sq_off is None else sq_off
                bases = (so, so + 8192) if sq_offs is None else sq_offs
                st["t"] = t_sq8
                st["v"] = [[av_bf(bases[s_] + kc * 1024, 512) for kc in range(8)] for s_ in range(2)]

            def sq_blk(b):
                s_ = b % 2
                for kc in range(8):
                    act(st["v"][s_][kc], xT[:, kc, b * 512:(b + 1) * 512], AF.Square, [t_xT[kc][b]], [st["t"][s_][kc]])

            def smm_blk(b):
                s_ = b % 2
                for kc in range(8):
                    mm(ps[:, 7, :], onesB[:], st["v"][s_][kc], kc == 0, kc == 7, [t_ones, st["t"][s_][kc]], [bank(7)])
                act(lnv2[:, s_, :], ps[:, 7, :], AF.Ln, [bank(7)], [t_lnv2[s_]], scale=1.0 / D, bias=EPS)
                act(rstd2[:, s_, :], lnv2[:, s_, :], AF.Exp, [t_lnv2[s_]], [t_rstd2[s_]], scale=-0.5)

            def hook(b):
                if b >= 1:
                    smm_blk(b - 1)
                sq_blk(b)
                if slice_fn is None and b >= lag:
                    apply_fn(b - lag)
                if slice_fn is not None and post_fn is not None and b >= 2:
                    post_fn(b - 2)

            def finish():
                smm_blk(NB - 1)
                for b in range(NB - (lag if slice_fn is None else 2), NB):
                    apply_fn(b)

            def inter(b):
                if slice_fn is None or b < 2:
                    return []
                return [(lambda kc=kc: slice_fn(b - 2, kc)) for kc in range(8)]

            return {"begin": begin, "hook": hook, "finish": finish, "inter": inter, "defer": defer}

        ffn_state = {"n13": 0, "nab": 0, "ndn": 0}

        def ffn(l, gate_col, t_gate, extra=None, pre=None, gate_tiles=(), tail=None):
            extra = extra or {}
            w1v = w1_d[l].rearrange("(k p) f -> p k f", p=128)
            w3v = w3_d[l].rearrange("(k p) f -> p k f", p=128)
            gT = [av_bf(OFF_GT + i * 4096, 2048) for i in range(8)]
            w2s = [av_bf(OFF_W2 + i * 2048, 1024) for i in range(8)]
            sil = [av_f32(OFF_SIL + i * 2048, 512) for i in range(2)]

            def p_issue(pf):
                sl1, sl3 = ring_alloc2()
                wv1 = wslot_view(OFF_W13, sl1)
                wv3 = wslot_view(OFF_W13, sl3)
                gt = list(gate_tiles) if pf < 4 else []
                dma("pool", wv1, w1v[:, :, pf * 128:(pf + 2) * 128], gt, [t_w13[sl1]], "ws%d" % sl1)
                dma("pool", wv3, w3v[:, :, pf * 128:(pf + 2) * 128], gt, [t_w13[sl3]], "ws%d" % sl3)
                return (wv1, wv3, sl1, sl3)

            def p_run(pf, f0, ld):
                P.phase = "ffn%d" % l
                wv1, wv3, sl1, sl3 = ld
                for fc in range(2):
                    fi = pf + fc - f0
                    for blk in range(NB):
                        k = ffn_state["nab"]
                        ffn_state["nab"] += 1
                        ba, bb = k % 2, 2 + k % 2
                        for kc in range(8):
                            mm(ps[:, ba, :], wv1[:, kc, fc * 128:(fc + 1) * 128], hT[:, kc, blk * 512:(blk + 1) * 512],
                               kc == 0, kc == 7, [t_w13[sl1], t_hT[kc][blk]], [bank(ba)])
                        for kc in range(8):
                            mm(ps[:, bb, :], wv3[:, kc, fc * 128:(fc + 1) * 128], hT[:, kc, blk * 512:(blk + 1) * 512],
                               kc == 0, kc == 7, [t_w13[sl3], t_hT[kc][blk]], [bank(bb)])
                        act(sil[k % 2], ps[:, ba, :], AF.Silu, [bank(ba)], [t_sil[k % 2]])
                        tt("dve", gT[fi][:, blk * 512:(blk + 1) * 512], sil[k % 2], ps[:, bb, :], ALU.mult,
                           [t_sil[k % 2], bank(bb)], [t_gT[fi][blk]])

            def w2_loads(f0, ng):
                for fi in range(ng):
                    f = f0 + fi
                    dma("pool", w2s[fi], w2_d[l][f * 128:(f + 1) * 128, :], [], [t_w2[fi]], "w2s%d" % fi)

            def down(f0, ng):
                for d in range(8):
                    for blk in range(NB):
                        k = ffn_state["ndn"]
                        ffn_state["ndn"] += 1
                        bd = 4 + k % 2
                        for fi in range(ng):
                            mm(ps[:, bd, :], w2s[fi][:, d * 128:(d + 1) * 128], gT[fi][:, blk * 512:(blk + 1) * 512],
                               fi == 0, fi == ng - 1, [t_w2[fi], t_gT[fi][blk]], [bank(bd)])
                        xv = xT[:, d, blk * 512:(blk + 1) * 512]
                        stt(xv, ps[:, bd, :], der[:, gate_col + d:gate_col + d + 1], xv, ALU.mult, ALU.add,
                            [bank(bd), t_gate, t_xT[d][blk]], [t_xT[d][blk]])

            P.phase = "ffn%d" % l
            def down_blk(f0, ng, blk, inter=()):
                inter = list(inter)
                for d in range(8):
                    if d >= 1 and inter:
                        inter.pop(0)()
                    k = ffn_state["ndn"]
                    ffn_state["ndn"] += 1
                    bd = 4 + k % 2
                    for fi in range(ng):
                        mm(ps[:, bd, :], w2s[fi][:, d * 128:(d + 1) * 128], gT[fi][:, blk * 512:(blk + 1) * 512],
                           fi == 0, fi == ng - 1, [t_w2[fi], t_gT[fi][blk]], [bank(bd)])
                    xv = xT[:, d, blk * 512:(blk + 1) * 512]
                    stt(xv, ps[:, bd, :], der[:, gate_col + d:gate_col + d + 1], xv, ALU.mult, ALU.add,
                        [bank(bd), t_gate, t_xT[d][blk]], [t_xT[d][blk]])
                while inter:
                    inter.pop(0)()

            def run_tail(f0, ng):
                tail["begin"]()
                for blk in range(NB):
                    down_blk(f0, ng, blk, tail["inter"](blk))
                    tail["hook"](blk)
                if tail.get("defer"):
                    late[tail["defer"]] = tail["finish"]
                else:
                    tail["finish"]()

            jobs = []
            pi = 0
            for gi_, (f0, f1) in enumerate(GROUPS):
                ng = f1 - f0
                if f0 != 0:
                    jobs.append(("run", (lambda f0=f0, ng=ng: w2_loads(f0, ng))))
                if f0 == 0 and pre is not None:
                    jobs.append(("prefetch", 2))
                    jobs.append(("run", pre))
                for pf in range(f0, f1, 2):
                    jobs.append(("load", (lambda pf=pf: p_issue(pf)), (lambda ld, pf=pf, f0=f0: p_run(pf, f0, ld))))
                    if pf == 0:
                        jobs.append(("run", (lambda f0=f0, ng=ng: w2_loads(f0, ng))))
                    jobs.extend(extra.get(pi, []))
                    pi += 1
                if tail is not None and gi_ == len(GROUPS) - 1:
                    jobs.append(("run", (lambda f0=f0, ng=ng: run_tail(f0, ng))))
                else:
                    jobs.append(("run", (lambda f0=f0, ng=ng: down(f0, ng))))
            run_jobs(jobs)

        late = {}

        def mixer(pre=None):
            winv = win_d.rearrange("(k p) f -> p k f", p=128)
            woutv = wout_d.rearrange("(k p) f -> p k f", p=128)
            t_lr = P.tiles("lr", NB)
            t_rm = P.tiles("rm")
            t_qk = P.tiles("qk", 4, NB)
            t_mix = P.tiles("mix", 4, NB)
            t_ws = P.tiles("ws", 3)
            t_L = P.tiles("L", 2, NB)
            t_C = P.tiles("C", 2, NB)
            t_ex = P.tiles("ex", 3)
            t_un = [t_L, t_C, t_ex]
            if pre is not None:
                P.inherit(flat(t_ws), P.retire(arena_tiles))
            else:
                remap([t_lr, t_rm, t_qk, t_mix, t_ws, t_un])
            union_tiles = flat(t_un)

            def remap_union(new):
                tok = P.retire(union_tiles)
                P.inherit(flat(new), tok)
                for t in union_tiles:
                    arena_tiles.remove(t)
                union_tiles.clear()
                union_tiles.extend(flat(new))
                arena_tiles.extend(flat(new))

            lrT = av_bf(OFF_LR, 2048)
            rmask = av_bf(OFF_RM, 2048)
            qk = [av_bf(OFF_QK + i * 4096, 2048) for i in range(4)]
            mixT = [av_bf(OFF_MIX + i * 4096, 2048) for i in range(4)]
            wsl = [wslot_view(OFF_WS, i) for i in range(3)]
            wsl2 = [av_bf(OFF_WS + i * 4096, 2048).rearrange("p (k f) -> p k f", f=512) for i in range(3)]
            wstate = {"n": 0, "issued": 0}
            wplan = []
            if not debug:
                wplan.append(("in", [(1536, 32, 0)]))
                for p_ in range(2):
                    wplan.append(("in", [(p_ * 128, 128, 0), (256 + p_ * 128, 128, 128)]))
                    wplan.append(("in", [(512 + p_ * 256, 256, 0)]))
                    wplan.append(("in", [(1024 + p_ * 256, 256, 0)]))
                    for s_ in range(20 + 8 * p_, 28 + 8 * p_):
                        wplan.append(("ada", s_))
                for dq_ in range(4):
                    wplan.append(("out", 0, dq_))
                for j_ in range(4):
                    wplan.append(("in", [(1568 + j_ * 128, 128, 0), (2080 + j_ * 128, 128, 128)]))
                    wplan.append(("in", [(2592 + j_ * 128, 128, 0)]))
                for half_ in range(2):
                    wplan.append(("out2", 1, half_))

            def w_issue(i, entry):
                s_ = i % 3
                if entry[0] == "ada":
                    dma("pool", wsl[s_], wada_v[:, :, entry[1] * 256:(entry[1] + 1) * 256], [], [t_ws[s_]], "ws%d" % s_)
                elif entry[0] == "in":
                    for (c0, ncl, do) in entry[1]:
                        dma("pool", wsl[s_][:, :, do:do + ncl], winv[:, :, c0:c0 + ncl], [], [t_ws[s_]], "ws%d" % s_)
                elif entry[0] == "out2":
                    _, part, half = entry
                    dma("pool", wsl2[s_], woutv[:, part * 4:(part + 1) * 4, half * 512:(half + 1) * 512], [],
                        [t_ws[s_]], "ws%d" % s_)
                else:
                    _, part, dq = entry
                    dma("pool", wsl[s_][:, 0:4, :], woutv[:, part * 4:(part + 1) * 4, dq * 256:(dq + 1) * 256], [],
                        [t_ws[s_]], "ws%d" % s_)

            def wnext(entry, hold=0):
                i = wstate["n"]
                wstate["n"] += 1
                if wplan:
                    assert wplan[i] == entry, (i, wplan[i], entry)
                    while wstate["issued"] < min(i + 3 - hold, len(wplan)):
                        w_issue(wstate["issued"], wplan[wstate["issued"]])
                        wstate["issued"] += 1
                else:
                    w_issue(i, entry)
                return (wsl2 if entry[0] == "out2" else wsl)[i % 3], t_ws[i % 3]

            def wload(cols, hold=0):
                return wnext(("in", cols), hold)

            def m1job(s_):
                wv_, wt_ = wnext(("ada", s_))
                flush_pending()
                for kc in range(8):
                    mm(ps[0:1, 6, 0:256], scB[:, kc:kc + 1], wv_[:, kc, :], kc == 0, kc == 7, [wt_, t_sc], [bank(6)])
                act(stA[0:1, :], ps[0:1, 6, 0:128], AF.Copy, [bank(6)], [t_stA])
                act(stB[0:1, :], ps[0:1, 6, 128:256], AF.Copy, [bank(6)], [t_stB])
                j0 = 2 * s_

                def k1():
                    mm(ps[:, 7, j0:j0 + 1], stA[0:1, :], identF[0:1, 0:1], True, True, [t_stA, t_cst], [bank(7)])
                    mm(ps[:, 7, j0 + 1:j0 + 2], stB[0:1, :], identF[0:1, 0:1], True, True, [t_stB, t_cst], [bank(7)])
                    tt("dve", modT[:, j0:j0 + 2], ps[:, 7, j0:j0 + 2], vA[:, j0:j0 + 2], ALU.add, [bank(7), t_vA],
                       [t_modi[j0 // 8]])
                pending.append(k1)

            def blkv(ap, blk):
                return ap[:, blk * 512:(blk + 1) * 512]

            while wplan and wstate["issued"] < 3:
                w_issue(wstate["issued"], wplan[wstate["issued"]])
                wstate["issued"] += 1
            if pre is not None:
                pre()
                remap([t_lr, t_rm, t_qk, t_mix, t_un])
                arena_tiles.extend(flat(t_ws))
            memset("pool", rmask, 1.0, [t_rm])
            memset("pool", rmask[:, 0:T:128], 0.0, [t_rm])

            P.phase = "mix_lr"
            wv, wt = wload([(1536, 32, 0)])
            for blk in range(NB):
                b = blk % 2
                for kc in range(8):
                    mm(ps[0:32, b, :], wv[:, kc, 0:32], blkv(hT[:, kc, :], blk), kc == 0, kc == 7,
                       [wt, t_hT[kc][blk]], [bank(b)])
                cp("act", blkv(lrT[0:32, :], blk), ps[0:32, b, :], [bank(b)], [t_lr[blk]])

            if debug == "m0":
                raise StopBuild()
            for p in (range(2) if debug != "conv" else []):
                P.phase = "prep%d" % p
                if debug == "m4" and p == 1:
                    raise StopBuild()
                if p == 1:
                    t_L2 = P.tiles("L", 2, NB)
                    t_C2 = P.tiles("C", 2, NB)
                    t_ex2 = P.tiles("ex", 3)
                    remap_union([t_L2, t_C2, t_ex2])
                    t_Lp, t_Cp, t_exp = t_L2, t_C2, t_ex2
                else:
                    t_Lp, t_Cp, t_exp = t_L, t_C, t_ex
                Lb = [av_f32(OFF_L + i * 8192, 2048) for i in range(2)]
                Cb = [av_f32(OFF_C + i * 8192, 2048) for i in range(2)]
                exb = [av_f32(OFF_EX + i * 2048, 512) for i in range(3)]
                nex = [0]

                def exslot():
                    i = nex[0] % 3
                    nex[0] += 1
                    return exb[i], t_exp[i]

                for blk in range(NB):
                    for dr in range(2):
                        b = 2 + (blk * 2 + dr) % 2
                        mm(ps[:, b, :], Wd[0:32, dr * 2 + p, :], blkv(lrT[0:32, :], blk), True, True,
                           [t_Wd, t_lr[blk]], [bank(b)])
                        ev, et = exslot()
                        act(ev, ps[:, b, :], AF.Exp, [bank(b), t_derC], [et], scale=-1.0,
                            bias=der[:, NBD + dr * 2 + p:NBD + dr * 2 + p + 1])
                        act(blkv(Lb[dr], blk), ev, AF.Ln, [et], [t_Lp[dr][blk]], bias=1.0)
                        P.add("dve", (lambda dr=dr, blk=blk: lambda e: e.tensor_tensor_scan(
                            out=blkv(Cb[dr], blk), data0=blkv(rmask, blk), data1=blkv(Lb[dr], blk), initial=0.0,
                            op0=ALU.mult, op1=ALU.add))(),
                            reads=flat([t_rm, t_Lp[dr][blk]]), writes=flat([t_Cp[dr][blk]]))
                    c3 = blkv(Cb[0], blk).rearrange("p (c t) -> p c t", t=128)
                    tt("dve", blkv(Lb[0], blk).rearrange("p (c t) -> p c t", t=128),
                       c3[:, :, 127:128].to_broadcast([128, 4, 128]), c3,
                       ALU.subtract, [t_Cp[0][blk]], [t_Lp[0][blk]])
                    tt("dve", blkv(Lb[1], blk), blkv(Cb[1], blk), blkv(Lb[1], blk), ALU.subtract,
                       [t_Cp[1][blk], t_Lp[1][blk]], [t_Lp[1][blk]])
                for dr in range(2):
                    act(eaT[:, dr, :], Cb[dr][:, 127:T:128], AF.Exp, [t_Cp[dr]], [t_ea], scale=-1.0 / 16)
                ts("dve", eaT[:, 0, 0:16:2], eaT[:, 0, 0:16:2], flg[:, 0:1], None, ALU.mult, None, [t_ea, t_flg], [t_ea])
                ts("dve", eaT[:, 1, 1:16:2], eaT[:, 1, 1:16:2], flg[:, 0:1], None, ALU.mult, None, [t_ea, t_flg], [t_ea])

                if debug == "m1":
                    raise StopBuild()
                P.phase = "qkvg%d" % p
                wv, wt = wload([(p * 128, 128, 0), (256 + p * 128, 128, 128)])
                for blk in range(NB):
                    bq, bk = blk % 2, 2 + blk % 2
                    for kc in range(8):
                        mm(ps[:, bq, :], wv[:, kc, 0:128], blkv(hT[:, kc, :], blk), kc == 0, kc == 7,
                           [wt, t_hT[kc][blk]], [bank(bq)])
                    for kc in range(8):
                        mm(ps[:, bk, :], wv[:, kc, 128:256], blkv(hT[:, kc, :], blk), kc == 0, kc == 7,
                           [wt, t_hT[kc][blk]], [bank(bk)])
                    for dr in range(2):
                        sgn = 1.0 / 16
                        eq, etq = exslot()
                        act(eq, blkv(Lb[dr], blk), AF.Exp, [t_Lp[dr][blk]], [etq], scale=sgn)
                        stt(blkv(qk[2 * dr], blk), ps[:, bq, :], 0.125, eq, ALU.mult, ALU.mult,
                            [bank(bq), etq], [t_qk[2 * dr][blk]])
                        ek, etk = exslot()
                        act(ek, blkv(Lb[dr], blk), AF.Exp, [t_Lp[dr][blk]], [etk], scale=-sgn)
                        tt("dve", blkv(qk[2 * dr + 1], blk), ps[:, bk, :], ek, ALU.mult,
                           [bank(bk), etk], [t_qk[2 * dr + 1][blk]])

                t_kt = P.tiles("kt", NB)
                t_vt = P.tiles("vt", 16)
                t_sg = P.tiles("sg", 2, NB)
                t_xp = P.tiles("xp", 2, 16)
                t_at = P.tiles("at", 2, 2)
                t_X = P.tiles("X", 2, 4)
                remap_union([t_kt, t_vt, t_sg, t_xp, t_at, t_X])
                ktok = av_bf(OFF_KT, 4096).rearrange("p (c f) -> p c f", f=256)
                vtok = av_bf(OFF_VT, 4096).rearrange("p (c f) -> p c f", f=256)
                sg = [av_bf(OFF_SG + i * 4096, 2048) for i in range(2)]
                xpb = av_bf(OFF_XP, 4096).rearrange("p (r c e) -> p r c e", c=16, e=128)
                atb = av_bf(OFF_AT, 1024).rearrange("p (h s f) -> p h s f", s=2, f=256)
                Xr = av_f32(OFF_X, 1024).rearrange("p (r s e) -> p r s e", s=4, e=128)

                for blk in range(NB):
                    b = 4 + blk % 2
                    pb = ps[:, b, :].bitcast(BF16).rearrange("p (c f) -> p c f", f=256)
                    for cc in range(4):
                        c = blk * 4 + cc
                        for dr in range(2):
                            tr(pb[:, cc, dr * 128:(dr + 1) * 128], qk[2 * dr + 1][:, c * 128:(c + 1) * 128], identB[:],
                               [t_qk[2 * dr + 1][blk], t_idB], [bank(b)])
                    cp(alt_eng(), ktok[:, blk * 4:(blk + 1) * 4, :], pb, [bank(b)], [t_kt[blk]])

                wv, wt = wload([(512 + p * 256, 256, 0)])
                for c in range(16):
                    b = c % 4
                    for kc in range(8):
                        mm(ps[:, b, 0:256], hT[:, kc, c * 128:(c + 1) * 128], wv[:, kc, :], kc == 0, kc == 7,
                           [wt, t_hT[kc][c // 4]], [bank(b)])
                    cp(alt_eng(), vtok[:, c, :], ps[:, b, 0:256], [bank(b)], [t_vt[c]])

                wvg, wtg = wload([(1024 + p * 256, 256, 0)])

                def gproj(i):
                    hh, blk = i // NB, i % NB
                    b = i % 4
                    for kc in range(8):
                        mm(ps[:, b, :], wvg[:, kc, hh * 128:(hh + 1) * 128], blkv(hT[:, kc, :], blk), kc == 0, kc == 7,
                           [wtg, t_hT[kc][blk]], [bank(b)])
                    act(blkv(sg[hh], blk), ps[:, b, :], AF.Silu, [bank(b)], [t_sg[hh][blk]])

                if debug == "m2":
                    raise StopBuild()
                P.phase = "chain%d" % p
                batches = []
                orders = {1: list(range(15, -1, -1)), 0: list(range(16))}
                for i in range(0, 16, 4):
                    for dr in (1, 0):
                        batches.append((dr, orders[dr][i:i + 4]))
                chst = {1: [s0sb[:, 2 + p, :], t_s0, 0], 0: [s0sb[:, p, :], t_s0, 0]}

                PB = (4, 5, 6, 7)

                def pmm(bi):
                    dr, cs = batches[bi]
                    b = PB[bi % 4]
                    for qq, c in enumerate(cs):
                        for hh in range(2):
                            mm(ps[hh * 64:(hh + 1) * 64, b, qq * 128:(qq + 1) * 128],
                               ktok[:, c, dr * 128 + hh * 64:dr * 128 + (hh + 1) * 64], vtok[:, c, hh * 128:(hh + 1) * 128],
                               True, True, [t_kt[c // 4], t_vt[c]], [bank(b)])

                def chain_step(bi, qq):
                    dr, cs = batches[bi]
                    b = PB[bi % 4]
                    c = cs[qq]
                    if True:
                        cur_ap, cur_t, step = chst[dr]
                        ea = eaT[:, dr, c:c + 1]
                        act(xpb[:, dr, c, :], cur_ap, AF.Identity, [cur_t, t_ea], [t_xp[dr][c]], scale=ea)
                        ns = step % 4
                        stt(Xr[:, dr, ns, :], cur_ap, ea, ps[:, b, qq * 128:(qq + 1) * 128], ALU.mult, ALU.add,
                            [cur_t, t_ea, bank(b)], [t_X[dr][ns]])
                        chst[dr] = [Xr[:, dr, ns, :], t_X[dr][ns], step + 1]
                        seg_end = (c % 2 == 1) if dr == 0 else (c % 2 == 0)
                        if seg_end:
                            dma("sp", st_d[c // 2, dr, 2 * p:2 * p + 2].rearrange("h d e -> (h d) e"), Xr[:, dr, ns, :],
                                [t_X[dr][ns]], [], "stx%d_%d" % (dr, ns))

                for bi in range(4):
                    pmm(bi)
                for k in range(4):
                    for qq in range(4):
                        chain_step(2 * k, qq)
                        chain_step(2 * k + 1, qq)
                    gproj(2 * k)
                    gproj(2 * k + 1)
                    if 2 * k + 4 < len(batches):
                        pmm(2 * k + 4)
                        pmm(2 * k + 5)

                if debug == "m3":
                    raise StopBuild()
                P.phase = "core%d" % p
                SB = (0, 1, 4, 5)

                def sbank(c, hh):
                    return SB[(c % 2) * 2 + hh]

                def scores(c, hh):
                    if c >= 16:
                        return
                    for dr in range(2):
                        sbk = sbank(c, hh)
                        mm(ps[:, sbk, dr * 128:(dr + 1) * 128],
                           qk[2 * dr + 1][hh * 64:(hh + 1) * 64, c * 128:(c + 1) * 128],
                           qk[2 * dr][hh * 64:(hh + 1) * 64, c * 128:(c + 1) * 128], True, True,
                           [t_qk[2 * dr + 1][c // 4], t_qk[2 * dr][c // 4]], [bank(sbk)])

                def maskpv(c):
                    blk = c // 4
                    scores(c + 1, 0)
                    scores(c + 1, 1)
                    for hh in range(2):
                        sbk = sbank(c, hh)
                        tt("dve", atb[:, hh, c % 2, :], ps[:, sbk, 0:256], maskFB, ALU.mult,
                           [bank(sbk), t_cst], [t_at[hh][c % 2]])
                    for hh in range(2):
                        ob = 2 + hh
                        ov = ps[:, ob, (c % 4) * 128:(c % 4 + 1) * 128]
                        vv = vtok[:, c, hh * 128:(hh + 1) * 128]
                        mm(ov, vv, atb[:, hh, c % 2, 0:128], True, False, [t_vt[c], t_at[hh][c % 2]], [bank(ob)])
                        mm(ov, vv, atb[:, hh, c % 2, 128:256], False, False, [t_vt[c], t_at[hh][c % 2]], [bank(ob)])
                        for dr in range(2):
                            mm(ov, xpb[hh * 64:(hh + 1) * 64, dr, c, :], qk[2 * dr][hh * 64:(hh + 1) * 64, c * 128:(c + 1) * 128],
                               False, dr == 1, [t_xp[dr][c], t_qk[2 * dr][blk]], [bank(ob)])
                    if c % 4 == 3:
                        for hh in range(2):
                            ob = 2 + hh
                            act(tmpr[:, hh, :], ps[:, ob, :], AF.Copy, [bank(ob)], [t_tmp[hh]])
                            act(sqr[:, hh, :], tmpr[:, hh, :], AF.Square, [t_tmp[hh]], [t_sq[hh]])
                        deferred.append((c + 2, blk))

                def onorm_tail(blk):
                    for hh in range(2):
                        sbk = 7 - hh
                        mm(ps[:, sbk, :], onesB[:], sqr[:, hh, :], True, True, [t_ones, t_sq[hh]], [bank(sbk)])
                    for hh in range(2):
                        sbk = 7 - hh
                        act(lnv2[:, hh, :], ps[:, sbk, :], AF.Ln, [bank(sbk)], [t_lnv2[hh]], scale=1.0 / 128, bias=EPS)
                    for hh in range(2):
                        act(rstd2[:, hh, :], lnv2[:, hh, :], AF.Exp, [t_lnv2[hh]], [t_rstd2[hh]], scale=-0.5)
                    for hh in range(2):
                        tt("dve", tmpr[:, hh, :], tmpr[:, hh, :], rstd2[:, hh, :], ALU.mult, [t_tmp[hh], t_rstd2[hh]], [t_tmp[hh]])
                    for hh in range(2):
                        hg = p * 2 + hh
                        stt(blkv(mixT[hg], blk), tmpr[:, hh, :], vB[:, GLN + hg:GLN + hg + 1], blkv(sg[hh], blk),
                            ALU.mult, ALU.mult, [t_tmp[hh], t_vB, t_sg[hh][blk]], [t_mix[hg][blk]])

                deferred = []
                scores(0, 0)
                scores(0, 1)
                for c in range(16):
                    maskpv(c)
                    while deferred and deferred[0][0] <= c:
                        onorm_tail(deferred.pop(0)[1])
                    if c % 2 == 1 and not debug:
                        m1job(20 + 8 * p + c // 2)
                while deferred:
                    onorm_tail(deferred.pop(0)[1])
                flush_pending()
                if p == 1 and not debug:
                    derive_d()

            def wout_part(part, t_mixp):
                P.phase = "wout%d" % part
                for dq in range(4):
                    wv_, wt_ = wnext(("out", part, dq))
                    for dd in range(2):
                        d = dq * 2 + dd
                        for blk in range(NB):
                            k = ffn_state["ndn"]
                            ffn_state["ndn"] += 1
                            bd = 4 + k % 4
                            for kk in range(4):
                                mm(ps[:, bd, :], wv_[:, kk, dd * 128:(dd + 1) * 128], blkv(mixT[kk], blk), kk == 0, kk == 3,
                                   [wt_, t_mixp[kk][blk]], [bank(bd)])
                            xv = xT[:, d, blk * 512:(blk + 1) * 512]
                            stt(xv, ps[:, bd, :], modT[:, 40 + d:41 + d], xv, ALU.mult, ALU.add,
                                [bank(bd), t_modi[5], t_xT[d][blk]], [t_xT[d][blk]])

            def wout_part_tail(part, t_mixp, tail):
                P.phase = "wout%d" % part
                wvA, wtA = wnext(("out2", part, 0))
                wvB, wtB = wnext(("out2", part, 1), hold=1)
                tail["begin"]()
                for blk in range(NB):
                    fill = list(tail["inter"](blk))
                    for d in range(8):
                        if d >= 1 and fill:
                            fill.pop(0)()
                        wv_, wt_ = (wvA, wtA) if d < 4 else (wvB, wtB)
                        dd = d % 4
                        k = ffn_state["ndn"]
                        ffn_state["ndn"] += 1
                        bd = 4 + k % 2
                        for kk in range(4):
                            mm(ps[:, bd, :], wv_[:, kk, dd * 128:(dd + 1) * 128], blkv(mixT[kk], blk), kk == 0, kk == 3,
                               [wt_, t_mixp[kk][blk]], [bank(bd)])
                        xv = xT[:, d, blk * 512:(blk + 1) * 512]
                        stt(xv, ps[:, bd, :], modT[:, 40 + d:41 + d], xv, ALU.mult, ALU.add,
                            [bank(bd), t_modi[5], t_xT[d][blk]], [t_xT[d][blk]])
                    while fill:
                        fill.pop(0)()
                    tail["hook"](blk)
                late["finish6"] = tail["finish"]

            if debug == "gla":
                for i in range(4):
                    for blk in range(NB):
                        cp("dve", xT[:, i, blk * 512:(blk + 1) * 512], blkv(mixT[i], blk), [t_mix[i][blk]], [t_xT[i][blk]])
                        cp("dve", xT[:, 4 + i, blk * 512:(blk + 1) * 512], blkv(qk[i], blk), [t_qk[i][blk]], [t_xT[4 + i][blk]])
                return
            if debug != "conv":
                wout_part(0, t_mix)

            P.phase = "conv"
            t_cv = P.tiles("cv", 3)
            remap_union([t_cv])
            cvb = [av_f32(OFF_CV + i * 2048, 512) for i in range(3)]
            for j in range(4):
                wvA, wtA = wload([(1568 + j * 128, 128, 0), (2080 + j * 128, 128, 128)])
                wvB, wtB = wload([(2592 + j * 128, 128, 0)], hold=1)
                for blk in range(NB):
                    b0 = 0 if (j * NB + blk) % 2 == 0 else 3
                    bcb, bcc, bch = b0, b0 + 1, b0 + 2
                    for (bb, wvx, wtx, co) in ((bcb, wvA, wtA, 0), (bcc, wvA, wtA, 128), (bch, wvB, wtB, 0)):
                        for kc in range(8):
                            mm(ps[:, bb, :], wvx[:, kc, co:co + 128], blkv(hT[:, kc, :], blk), kc == 0, kc == 7,
                               [wtx, t_hT[kc][blk]], [bank(bb)])
                    ccs, u, cu = cvb
                    act(ccs, ps[:, bcc, :], AF.Copy, [bank(bcc)], [t_cv[0]])
                    tt("dve", u, ccs, ps[:, bch, :], ALU.mult, [t_cv[0], bank(bch)], [t_cv[1]])
                    act(cu, u, AF.Identity, [t_cv[1], t_vB], [t_cv[2]], scale=vB[:, CW + 4 + j:CW + 5 + j])
                    u3 = u.rearrange("p (r t) -> p r t", t=64)
                    cu3 = cu.rearrange("p (r t) -> p r t", t=64)
                    stt(cu3[:, :, 1:64], u3[:, :, 0:63], vB[:, CW + j:CW + j + 1], cu3[:, :, 1:64], ALU.mult, ALU.add,
                        [t_cv[1], t_cv[2], t_vB], [t_cv[2]])
                    stt(cu3[:, :, 0:63], u3[:, :, 1:64], vB[:, CW + 8 + j:CW + 9 + j], cu3[:, :, 0:63], ALU.mult, ALU.add,
                        [t_cv[1], t_cv[2], t_vB], [t_cv[2]])
                    u4 = u.rearrange("p (s r t) -> p s r t", r=4, t=64)
                    cu4 = cu.rearrange("p (s r t) -> p s r t", r=4, t=64)
                    stt(cu4[:, :, 1:4, 0:1], u4[:, :, 0:3, 63:64], der[:, W0F + j:W0F + j + 1], cu4[:, :, 1:4, 0:1],
                        ALU.mult, ALU.add, [t_cv[1], t_cv[2], t_derC], [t_cv[2]])
                    stt(cu4[:, :, 0:3, 63:64], u4[:, :, 1:4, 0:1], der[:, W2F + j:W2F + j + 1], cu4[:, :, 0:3, 63:64],
                        ALU.mult, ALU.add, [t_cv[1], t_cv[2], t_derC], [t_cv[2]])
                    tt("dve", blkv(mixT[j], blk), cu, ps[:, bcb, :], ALU.mult, [t_cv[2], bank(bcb), t_mix[j][blk]],
                       [t_mix[j][blk]])
            if debug == "conv":
                for i in range(4):
                    for blk in range(NB):
                        cp("dve", xT[:, 4 + i, blk * 512:(blk + 1) * 512], blkv(mixT[i], blk), [t_mix[i][blk]], [t_xT[4 + i][blk]])
                return
            if debug:
                wout_part(1, t_mix)
            else:
                t_x6 = P.tiles("x6", 2)
                x6v = [av_f32(OFF_U + 24576 + i * 2048, 512) for i in range(2)]
                tmps6 = [(tmpr[:, 0, :], t_tmp[0]), (x6v[0], t_x6[0]), (tmpr[:, 1, :], t_tmp[1]), (x6v[1], t_x6[1])]
                wout_part_tail(1, t_mix, make_norm_tail(
                    lambda b: norm_apply(b, GM2, 6, t_derD, act_heavy=True, tmps=tmps6), "norm6",
                    sq_off=OFF_U + 8192, alias_fn=lambda t: remap_union([t, t_x6]),
                    slice_fn=lambda b, kc: norm_apply(b, GM2, 6, t_derD, act_heavy=True, tmps=tmps6, kcs=[kc])))

        fin = {"n": 0}

        def final_begin():
            t_yt = P.tiles("yt", 8)
            t_ys = P.tiles("ys", 2)
            tok = P.retire(flat(t_hT))
            P.inherit(flat([t_yt, t_ys]), tok)
            hflat = hT[:].rearrange("p k t -> p (k t)")
            fin["t_yt"], fin["t_ys"] = t_yt, t_ys
            fin["yt"] = [hflat[:, i * 1024:(i + 1) * 1024].bitcast(F32) for i in range(8)]
            fin["ys"] = [hflat[:, 8192 + i * 2048:8192 + (i + 1) * 2048].bitcast(F32) for i in range(2)]

        def final_apply(blk, part="all", kcs=None):
            t_yt, t_ys, yt, ys = fin["t_yt"], fin["t_ys"], fin["yt"], fin["ys"]
            if part in ("all", "stt"):
                for kc in (range(8) if kcs is None else kcs):
                    stt(yt[kc], xT[:, kc, blk * 512:(blk + 1) * 512], vB[:, FN + kc:FN + kc + 1], rstd2[:, blk % 2, :], ALU.mult, ALU.mult,
                        [t_xT[kc][blk], t_vB, t_rstd2[blk % 2]], [t_yt[kc]])
            if part in ("all", "rest"):
                n = fin["n"]
                for j in range(4):
                    s = n % 2
                    n += 1
                    for half in range(2):
                        b = (n * 2 + half) % 4
                        for kk in range(4):
                            tr(ps[:, b, kk * 128:(kk + 1) * 128], yt[half * 4 + kk][:, j * 128:(j + 1) * 128], identF,
                               [t_yt[half * 4 + kk], t_cst], [bank(b)])
                        cp("act", ys[s][:, half * 512:(half + 1) * 512], ps[:, b, :], [bank(b)], [t_ys[s]])
                    r0 = (blk * 4 + j) * 128
                    dma("sp", y_d[r0:r0 + 128, :], ys[s], [t_ys[s]], [], "y%d" % s)
                fin["n"] = n

        remap_dummy = None
        arena_tiles.extend(flat([t_w13]))
        t_ada = P.tiles("ada", 8)
        hflat0 = hT[:].rearrange("p k t -> p (k t)")
        ada_v = [hflat0[:, i * 2048:(i + 1) * 2048].rearrange("p (k f) -> p k f", f=256) for i in range(8)]
        for i in range(8):
            dma("pool", ada_v[i], wada_v[:, :, i * 256:(i + 1) * 256], [], [t_ada[i]], "ada%d" % i)

        def m_early(j0):
            views = [(ada_v[j0 // 2 + i], None) for i in range(2)]
            tl = [t_ada[j0 // 2 + i] for i in range(2)]
            k = mods_state["bank"]
            mods_state["bank"] += 1
            rs = k % 2
            flush_pending()
            for i, (wv, _) in enumerate(views):
                for kc in range(8):
                    mm(ps[0:1, 6, i * 256:(i + 1) * 256], scB[:, kc:kc + 1], wv[:, kc, :], kc == 0, kc == 7,
                       [tl[i], t_sc], [bank(6)])
            act(lnv2[0:1, rs, :], ps[0:1, 6, :], AF.Copy, [bank(6)], [t_lnv2[rs]])

            def k1():
                for jj in range(4):
                    j = j0 + jj
                    mm(ps[:, 7, j:j + 1], lnv2[0:1, rs, jj * 128:(jj + 1) * 128], identF[0:1, 0:1], True, True,
                       [t_lnv2[rs], t_cst], [bank(7)])
                tt("dve", modT[:, j0:j0 + 4], ps[:, 7, j0:j0 + 4], vA[:, j0:j0 + 4], ALU.add, [bank(7), t_vA],
                   [t_modi[j0 // 8]])
            pending.append(k1)

        in_transposes(0)
        m_early(0)
        in_transposes(1)
        m_early(4)
        in_transposes(2)
        m_early(8)
        m_early(12)
        flush_pending()
        derive_a()
        in_transposes(3)
        P.inherit(flat(t_hT), P.retire(t_ada))
        for dr in range(2):
            for p in range(2):
                dma("pool", Wd[dr * 16:(dr + 1) * 16, dr * 2 + p, :], wdec_d[dr, :, p * 128:(p + 1) * 128],
                    [], [t_Wd4[dr * 2 + p]], "c5_%d" % (dr * 2 + p))
        P.inherit(flat(t_gT), P.retire([t_xs[0], t_xs[1]]))
        P.inherit(flat(t_w2), P.retire([t_xs[2]]))
        P.inherit(flat(t_sil), P.retire([t_xs[3]]))
        for t in flat(t_xs):
            arena_tiles.remove(t)
        arena_tiles.extend(flat([t_gT, t_w2, t_sil]))
        extra = {0: [mjob(16), mjob(20), ("run", derive_b)], 1: [mjob(24)], 2: [mjob(28)], 3: [mjob(32)],
                 4: [mjob(36), ("run", derive_c)]}
        ffn(0, G1, t_derB, extra, pre=lambda: norm_mod(GM1, 0, t_derA, tmps=tmps4()), gate_tiles=[],
            tail=(make_norm_tail(lambda b: norm_apply(b, GMM, 3, t_derC, act_heavy=True, tmps=tmps4()), "norm3",
                                 slice_fn=lambda b, kc: norm_apply(b, GMM, 3, t_derC, act_heavy=True, tmps=tmps4(), kcs=[kc]),
                                 sq_offs=(OFF_W13, 77824), defer="finish3")
                  if debug != "ffn1" else None))
        if debug != "ffn1":
            try:
                mixer(pre=(lambda: late["finish3"]()))
            except StopBuild:
                pass
            if debug not in ("mixer", "gla", "conv", "m0", "m1", "m2", "m3", "m4"):
                t_w13b = P.tiles("w13", 6)
                t_w2b = P.tiles("w2s", 8)
                t_gTb = P.tiles("gT", 8, NB)
                t_silb = P.tiles("sil", 2)
                P.inherit(flat(t_w13b), P.retire(arena_tiles))
                t_w13[:] = t_w13b

                def pre2():
                    late["finish6"]()
                    remap([t_w2b, t_gTb, t_silb])
                    arena_tiles.extend(flat(t_w13b))
                    t_w2[:] = t_w2b
                    t_gT[:] = t_gTb
                    t_sil[:] = t_silb

                ffn(1, G2, t_derD, pre=pre2,
                    tail=make_norm_tail(final_apply, "final", begin_extra=final_begin, lag=2,
                                        slice_fn=lambda b, kc: final_apply(b, "stt", [kc]),
                                        post_fn=lambda b: final_apply(b, "rest")))
        if debug:
            for kc in range(8):
                dma("sp", dbg_d[kc], xT[:, kc, :], [t_xT[kc]], [], "dbg%d" % kc)

        P.finalize()
        nc._phase_list = [op.phase for op in P.ops if op.eng == "pe"]
        for k in P.dma_counts:
            getsem(("dma", k))
        out_keys = [k for k in P.dma_counts if k.startswith("y") or k.startswith("stx") or k.startswith("dbg")]
        with nc.Block() as block:
            @block.tensor
            def _(h):
                emit_engine(P, "pe", h, sems)

            @block.scalar
            def _(h):
                emit_engine(P, "act", h, sems)

            @block.vector
            def _(h):
                emit_engine(P, "dve", h, sems)

            @block.gpsimd
            def _(h):
                emit_engine(P, "pool", h, sems)

            @block.sync
            def _(h):
                emit_engine(P, "sp", h, sems)
                for k in out_keys:
                    h.wait_ge(sems[("dma", k)], 16 * P.dma_counts[k])
    return nc


def _core_inputs(inp):
    f = lambda a: np.ascontiguousarray(np.asarray(a, dtype=np.float32))
    xp = f(inp["x_prompt"])
    xsm = f(inp["x_sample"])
    sg = f(inp["state_gla"])
    c = f(inp["c"])
    cctx = f(inp["c_ctx"])
    vecB = np.concatenate([
        f(inp["norm_ffn1"]).reshape(8, 128), f(inp["norm_mix"]).reshape(8, 128),
        f(inp["norm_ffn2"]).reshape(8, 128), f(inp["final_norm"]).reshape(8, 128),
        f(inp["gla_norm"]).reshape(4, 128), f(inp["conv_w"]).reshape(12, 128),
        f(inp["b_decay"]).reshape(4, 128)], axis=0)
    ident = np.eye(128, dtype=np.float32)
    s_i = np.arange(128)[:, None]
    t_i = np.arange(128)[None, :]
    cst = np.concatenate([ident, (s_i <= t_i).astype(np.float32), (s_i >= t_i).astype(np.float32)], axis=1)
    shared = {
        "vecB": np.ascontiguousarray(vecB), "cst": np.ascontiguousarray(cst),
        "w_ada": f(inp["w_ada"])[0],
        "w1_ffn1": f(inp["w1_ffn1"])[0], "w3_ffn1": f(inp["w3_ffn1"])[0], "w2_ffn1": f(inp["w2_ffn1"])[0],
        "w1_ffn2": f(inp["w1_ffn2"])[0], "w3_ffn2": f(inp["w3_ffn2"])[0], "w2_ffn2": f(inp["w2_ffn2"])[0],
        "w_in": f(inp["w_in"])[0], "w_decay": f(inp["w_decay"])[0], "w_out": f(inp["w_out"])[0],
    }
    b_ada = f(inp["b_ada"]).reshape(72, 128)
    maps = []
    for ci in range(8):
        m = dict(shared)
        if ci < 4:
            m["x"] = np.ascontiguousarray(xsm[ci])
            cv = c[ci]
            m["s0"] = np.ascontiguousarray(sg[ci, 0])
            fl = np.array([1.0, 0.0], np.float32)
        else:
            j = ci - 4
            m["x"] = np.ascontiguousarray(xp[8 * j:8 * j + 8].reshape(T, D))
            cv = cctx
            m["s0"] = np.zeros((2, 4, 64, 128), np.float32)
            fl = np.array([0.0, 1.0], np.float32)
        m["vecA"] = np.ascontiguousarray(np.concatenate([b_ada, cv.reshape(8, 128)], axis=0))
        m["flags"] = np.ascontiguousarray(np.broadcast_to(fl[None, :], (128, 2)))
        maps.append(m)
    return maps


_NC_CACHE = {}


def kernel(**inputs):
    maps = _core_inputs(inputs)
    if "nc" not in _NC_CACHE:
        _NC_CACHE["nc"] = build_program()
    nc = _NC_CACHE["nc"]
    res = run_bass_kernel_spmd(nc, maps, core_ids=list(range(8)))
    outs = res.results
    y_sample = np.stack([np.asarray(outs[ci]["y"], dtype=np.float32) for ci in range(4)], axis=0)
    y_prompt = np.concatenate([np.asarray(outs[ci]["y"], dtype=np.float32).reshape(8, 256, D) for ci in range(4, 8)], axis=0)
    st = np.concatenate([np.asarray(outs[ci]["st"], dtype=np.float32) for ci in range(4, 8)], axis=0)
    new_state = np.ascontiguousarray(st.reshape(32, 1, 2, 4, 64, 128))
    return (y_prompt, y_sample, new_state)
```

```python
import contextlib
import numpy as np
import concourse.bass as bass
import concourse.mybir as mybir
from concourse.bass_utils import run_bass_kernel_spmd

F32 = mybir.dt.float32
BF16 = mybir.dt.bfloat16
AF = mybir.ActivationFunctionType
ALU = mybir.AluOpType

ENGS = ("pe", "act", "dve", "pool", "sp")
T = 2048
NB = 4
D = 1024
DFF = 2816
NFC = 22
EPS = 1e-6
GROUPS = [(0, 8), (8, 16), (16, 22)]


class Tile:
    __slots__ = ("name", "writer", "readers", "excl")

    def __init__(self, name):
        self.name = name
        self.writer = None
        self.readers = {}
        self.excl = False


class Op:
    __slots__ = ("eng", "fn", "deps", "dma", "sem", "val", "signal", "idx", "phase")

    def __init__(self, eng, fn, dma, idx):
        self.eng = eng
        self.fn = fn
        self.deps = []
        self.dma = dma
        self.sem = None
        self.val = 0
        self.signal = False
        self.idx = idx


class Prog:
    def __init__(self):
        self.ops = []
        self.dma_counts = {}
        self.phase = "init"

    def tiles(self, name, *dims):
        if len(dims) == 0:
            return Tile(name)
        return [self.tiles(f"{name}_{i}", *dims[1:]) for i in range(dims[0])]

    def add(self, eng, fn, reads=(), writes=(), dma_key=None):
        op = Op(eng, fn, dma_key is not None, len(self.ops))
        op.phase = self.phase
        deps = {}
        for t in reads:
            if t.writer is not None:
                deps[id(t.writer)] = t.writer
            if t.excl:
                for k, r in t.readers.items():
                    if k != eng:
                        deps[id(r)] = r
        for t in writes:
            if t.writer is not None:
                deps[id(t.writer)] = t.writer
            for r in t.readers.values():
                deps[id(r)] = r
        op.deps = list(deps.values())
        for t in reads:
            if op.dma:
                t.readers[("dma", len(t.readers))] = op
            else:
                t.readers[eng] = op
        for t in writes:
            t.writer = op
            t.readers = {}
        if dma_key is not None:
            c = self.dma_counts.get(dma_key, 0) + 1
            self.dma_counts[dma_key] = c
            op.sem = ("dma", dma_key)
            op.val = 16 * c
            op.signal = True
        self.ops.append(op)
        return op

    def retire(self, tiles):
        tok = {}
        for t in tiles:
            cands = list(t.readers.values())
            if t.writer is not None:
                cands.append(t.writer)
            for o in cands:
                if o.dma:
                    tok[("dma", id(o))] = o
                else:
                    kk = ("c", o.eng)
                    if kk not in tok or tok[kk].idx < o.idx:
                        tok[kk] = o
        return tok

    def inherit(self, tiles, tok):
        for t in tiles:
            t.writer = None
            t.readers = dict(tok)

    def finalize(self):
        for op in self.ops:
            for d in op.deps:
                if not d.dma:
                    if d.eng == "pe" and op.eng == "pe":
                        continue
                    d.signal = True
        counts = {e: 0 for e in ENGS}
        for op in self.ops:
            if not op.dma and op.signal:
                counts[op.eng] += 1
                op.sem = ("eng", op.eng)
                op.val = counts[op.eng]


def emit_engine(prog, eng, handle, sems):
    waited = {}
    for op in prog.ops:
        if op.eng != eng:
            continue
        need = {}
        for d in op.deps:
            if d.sem is None:
                continue
            if d.eng == "pe" and eng == "pe" and not d.dma:
                continue
            if d.val > need.get(d.sem, 0):
                need[d.sem] = d.val
        for k, v in need.items():
            if v > waited.get(k, 0):
                handle.wait_ge(sems[k], v)
                waited[k] = v
        ins = op.fn(handle)
        if op.signal:
            ins.then_inc(sems[op.sem], 16 if op.dma else 1)


class StopBuild(Exception):
    pass


def flat(x):
    out = []
    if isinstance(x, (list, tuple)):
        for y in x:
            out.extend(flat(y))
    else:
        out.append(x)
    return out


def build_program(debug=None):
    nc = bass.Bass("TRN2", target_bir_lowering=False)
    P = Prog()
    es = contextlib.ExitStack()

    def dram(name, shape, dt=F32, out=False):
        return nc.dram_tensor(name, shape, dt, kind="ExternalOutput" if out else "ExternalInput").ap()

    x_d = dram("x", [T, D])
    vecA_d = dram("vecA", [80, 128])
    vecB_d = dram("vecB", [52, 128])
    s0_d = dram("s0", [2, 4, 64, 128])
    flags_d = dram("flags", [128, 2])
    cst_d = dram("cst", [128, 384])
    wada_d = dram("w_ada", [D, 9 * D])
    w1_d = [dram("w1_ffn1", [D, DFF]), dram("w1_ffn2", [D, DFF])]
    w3_d = [dram("w3_ffn1", [D, DFF]), dram("w3_ffn2", [D, DFF])]
    w2_d = [dram("w2_ffn1", [DFF, D]), dram("w2_ffn2", [DFF, D])]
    win_d = dram("w_in", [D, 3104])
    wdec_d = dram("w_decay", [2, 16, 256])
    wout_d = dram("w_out", [D, D])
    y_d = dram("y", [T, D], out=True)
    st_d = dram("st", [8, 2, 4, 64, 128], out=True)
    dbg_d = dram("dbg", [8, 128, T], out=True) if debug else None

    with es:
        def sb(name, shape, dt):
            return es.enter_context(nc.sbuf_tensor("sb_" + name, shape, dt))

        xT = sb("xT", [128, 8, T], F32)
        hT = sb("hT", [128, 8, T], BF16)
        NA = 46080
        arena = sb("arena", [128, NA], BF16)
        sqr = sb("sqr", [128, 2, 512], BF16)
        tmpr = sb("tmpr", [128, 2, 512], F32)
        lnv2 = sb("lnv", [128, 2, 512], F32)
        rstd2 = sb("rstd", [128, 2, 512], F32)
        cst = sb("cst", [128, 384], F32)
        identB = sb("identB", [128, 128], BF16)
        onesB = sb("onesB", [128, 128], BF16)
        stA = sb("stA", [128, 128], F32)
        stB = sb("stB", [128, 128], F32)
        vA = sb("vA", [128, 80], F32)
        vB = sb("vB", [128, 52], F32)
        modT = sb("modT", [128, 72], F32)
        der = sb("der", [128, 64], F32)
        flg = sb("flg", [128, 2], F32)
        scB = sb("scB", [128, 8], BF16)
        s0sb = sb("s0sb", [128, 4, 128], F32)
        Wd = sb("Wd", [32, 4, 128], BF16)
        eaT = sb("eaT", [128, 2, 16], F32)
        ps = es.enter_context(nc.psum_tensor("ps", [128, 8, 512], F32))

        identF = cst[:, 0:128]
        maskFB = cst[:, 128:384]

        def av_bf(off, n):
            return arena[:, off // 2: off // 2 + n]

        def av_f32(off, n):
            return arena[:, off // 2: off // 2 + 2 * n].bitcast(F32)

        sems = {}

        def getsem(k):
            if k not in sems:
                sems[k] = es.enter_context(nc.semaphore("s%d" % len(sems)))
            return sems[k]

        for e in ENGS:
            getsem(("eng", e))

        t_xT = P.tiles("xT", 8, NB)
        t_hT = P.tiles("hT", 8, NB)
        t_ps = P.tiles("ps", 8)
        for t in t_ps:
            t.excl = True
        t_sq = P.tiles("sq", 2)
        t_tmp = P.tiles("tmp", 2)
        t_lnv2 = P.tiles("lnv", 2)
        t_rstd2 = P.tiles("rstd", 2)
        t_cst = P.tiles("cst")
        t_idB = P.tiles("idB")
        t_ones = P.tiles("ones")
        t_stA = P.tiles("stA")
        t_stB = P.tiles("stB")
        t_vA = P.tiles("vA")
        t_vB = P.tiles("vB")
        t_modi = P.tiles("mod", 9)
        t_mod = t_modi
        t_derA = P.tiles("derA")
        t_derB = P.tiles("derB")
        t_derC = P.tiles("derC")
        t_derD = P.tiles("derD")
        t_der = [t_derA, t_derB, t_derC, t_derD]
        t_flg = P.tiles("flg")
        t_sc = P.tiles("sc")
        t_s0 = P.tiles("s0")
        t_Wd4 = P.tiles("Wd", 4)
        t_Wd = t_Wd4
        t_ea = P.tiles("ea")
        t_y = P.tiles("y")
        t_st = P.tiles("st")

        def bank(i):
            return t_ps[i]

        def dma(q, out, in_, reads, writes, key):
            getsem(("dma", key))
            return P.add(q, lambda e: e.dma_start(out=out, in_=in_), reads=flat(reads), writes=flat(writes), dma_key=key)

        def mm(out, lhsT, rhs, start, stop, reads, writes):
            return P.add("pe", lambda e: e.matmul(out, lhsT, rhs, start=start, stop=stop),
                         reads=flat(reads), writes=flat(writes))

        def tr(out, in_, ident, reads, writes):
            return P.add("pe", lambda e: e.transpose(out=out, in_=in_, identity=ident),
                         reads=flat(reads), writes=flat(writes))

        def act(out, in_, func, reads, writes, scale=None, bias=None):
            kw = {}
            if scale is not None:
                kw["scale"] = scale
            if bias is not None:
                kw["bias"] = bias
            return P.add("act", lambda e: e.activation(out=out, in_=in_, func=func, **kw),
                         reads=flat(reads), writes=flat(writes))

        def tt(eng, out, in0, in1, op, reads, writes):
            return P.add(eng, lambda e: e.tensor_tensor(out=out, in0=in0, in1=in1, op=op),
                         reads=flat(reads), writes=flat(writes))

        def ts(eng, out, in0, s1, s2, op0, op1, reads, writes):
            if s2 is None:
                return P.add(eng, lambda e: e.tensor_scalar(out=out, in0=in0, scalar1=s1, scalar2=None, op0=op0),
                             reads=flat(reads), writes=flat(writes))
            return P.add(eng, lambda e: e.tensor_scalar(out=out, in0=in0, scalar1=s1, scalar2=s2, op0=op0, op1=op1),
                         reads=flat(reads), writes=flat(writes))

        def stt(out, in0, scalar, in1, op0, op1, reads, writes):
            return P.add("dve", lambda e: e.scalar_tensor_tensor(out=out, in0=in0, scalar=scalar, in1=in1, op0=op0, op1=op1),
                         reads=flat(reads), writes=flat(writes))

        def cp(eng, out, in_, reads, writes):
            if eng == "act":
                return act(out, in_, AF.Copy, reads, writes)
            return P.add(eng, lambda e: e.tensor_copy(out=out, in_=in_), reads=flat(reads), writes=flat(writes))

        def memset(eng, out, val, writes):
            return P.add(eng, lambda e: e.memset(out, val), writes=flat(writes))

        cnt = {"evac": 0}

        def alt_eng():
            cnt["evac"] += 1
            return "act" if cnt["evac"] % 2 else "dve"

        OFF_GT = 0
        OFF_W13 = OFF_GT + 8 * 4096
        OFF_W2 = OFF_W13 + 6 * 4096
        OFF_SIL = OFF_W2 + 8 * 2048
        assert OFF_SIL + 2 * 2048 <= NA * 2
        OFF_XS = 0
        OFF_YT = 0
        OFF_YS = 8 * 2048
        OFF_LR = 0
        OFF_RM = OFF_LR + 4096
        OFF_QK = OFF_RM + 4096
        OFF_MIX = OFF_QK + 16384
        OFF_WS = OFF_MIX + 16384
        OFF_U = OFF_WS + 3 * 4096
        OFF_L = OFF_U
        OFF_C = OFF_L + 16384
        OFF_EX = OFF_C + 16384
        OFF_KT = OFF_U
        OFF_VT = OFF_KT + 8192
        OFF_SG = OFF_VT + 8192
        OFF_XP = OFF_SG + 8192
        OFF_AT = OFF_XP + 8192
        OFF_X = OFF_AT + 2048
        assert OFF_X + 4096 <= OFF_U + 38912 and OFF_EX + 6144 <= OFF_U + 38912
        assert OFF_U + 38912 <= NA * 2
        OFF_CV = OFF_U

        arena_tiles = []

        def remap(new_tiles):
            tok = P.retire(arena_tiles)
            P.inherit(flat(new_tiles), tok)
            arena_tiles.clear()
            arena_tiles.extend(flat(new_tiles))

        dma("sp", cst[:], cst_d, [], [t_cst], "c0")
        dma("sp", stA[0:80, :], vecA_d, [], [t_stA], "c1")
        dma("sp", stB[0:52, :], vecB_d, [], [t_stB], "c2")
        dma("sp", flg[:], flags_d, [], [t_flg], "c3")
        dma("sp", s0sb[:], s0_d.rearrange("r (a h) d e -> (h d) (r a) e", h=2), [], [t_s0], "c4")
        memset("pool", Wd[:], 0.0, t_Wd4)
        memset("dve", onesB[:], 1.0, [t_ones])
        cp("dve", identB[:], identF, [t_cst], [t_idB])

        t_xs = P.tiles("xs", 4)
        remap([t_xs])
        XS_OFF = (0, 16384, 57344, 73728)
        xs = [av_f32(XS_OFF[i], 4096).rearrange("p (j f) -> p j f", f=1024) for i in range(4)]

        def load_x(blk):
            dma("sp", xs[blk], x_d[blk * 512:(blk + 1) * 512, :].rearrange("(j p) f -> p j f", p=128),
                [], [t_xs[blk]], "xs%d" % blk)

        for blk_ in range(4):
            load_x(blk_)

        tr(ps[:, 7, 0:80], stA[0:80, :], identF[0:80, 0:80], [t_stA, t_cst], [bank(7)])
        tr(ps[:, 5, 0:52], stB[0:52, :], identF[0:52, 0:52], [t_stB, t_cst], [bank(5)])
        cp("dve", vA[:], ps[:, 7, 0:80], [bank(7)], [t_vA])
        cp("dve", vB[:], ps[:, 5, 0:52], [bank(5)], [t_vB])
        act(scB[:], vA[:, 72:80], AF.Silu, [t_vA], [t_sc])

        N1, NM, N2, FN, GLN, CW, BD = 0, 8, 16, 24, 32, 36, 48
        GM1, GMM, GM2, G1, G2, NBD, W0F, W2F = 0, 8, 16, 24, 32, 40, 44, 48

        def in_transposes(blk):
            for kc in range(8):
                b = kc % 4
                for j in range(4):
                    tr(ps[:, b, j * 128:(j + 1) * 128], xs[blk][:, j, kc * 128:(kc + 1) * 128], identF,
                       [t_xs[blk], t_cst], [bank(b)])
                cp(alt_eng(), xT[:, kc, blk * 512:(blk + 1) * 512], ps[:, b, :], [bank(b)], [t_xT[kc][blk]])

        def wslot_view(off, i):
            return av_bf(off + i * 4096, 2048).rearrange("p (k f) -> p k f", f=256)

        t_w13 = P.tiles("w13", 6)
        t_w2 = P.tiles("w2s", 8)
        t_gT = P.tiles("gT", 8, NB)
        t_sil = P.tiles("sil", 2)
        ffn_tiles = [t_w13, t_w2, t_gT, t_sil]

        wada_v = wada_d.rearrange("(k p) f -> p k f", p=128)

        mods_state = {"bank": 0}
        ring = {"n": 0}

        def ring_alloc2():
            n = ring["n"]
            ring["n"] += 2
            return n % 6, (n + 1) % 6

        def mjob_issue(j0):
            sl = ring_alloc2()
            views = []
            for i, s_ in enumerate(sl):
                wv = wslot_view(OFF_W13, s_)
                c0 = (j0 // 2 + i) * 256
                dma("pool", wv, wada_v[:, :, c0:c0 + 256], [], [t_w13[s_]], "ws%d" % s_)
                views.append((wv, s_))
            return views

        pending = []

        def flush_pending():
            while pending:
                pending.pop(0)()

        def mjob_run(j0, views):
            k = mods_state["bank"]
            mods_state["bank"] += 1
            rs = k % 2
            for i, (wv, s_) in enumerate(views):
                for kc in range(8):
                    mm(ps[0:1, 6, i * 256:(i + 1) * 256], scB[:, kc:kc + 1], wv[:, kc, :], kc == 0, kc == 7,
                       [t_w13[s_], t_sc], [bank(6)])
            act(lnv2[0:1, rs, :], ps[0:1, 6, :], AF.Copy, [bank(6)], [t_lnv2[rs]])

            def k1():
                for jj in range(4):
                    j = j0 + jj
                    mm(ps[:, 7, j:j + 1], lnv2[0:1, rs, jj * 128:(jj + 1) * 128], identF[0:1, 0:1], True, True,
                       [t_lnv2[rs], t_cst], [bank(7)])
                tt("dve", modT[:, j0:j0 + 4], ps[:, 7, j0:j0 + 4], vA[:, j0:j0 + 4], ALU.add, [bank(7), t_vA],
                   [t_modi[j0 // 8]])
            pending.append(k1)

        def run_jobs(jobs):
            loads = [j for j in jobs if j[0] == "load"]
            issued = {}
            st = {"n": 0}

            def ensure(upto):
                while st["n"] < min(upto, len(loads)):
                    issued[id(loads[st["n"]])] = loads[st["n"]][1]()
                    st["n"] += 1

            li = 0
            for j in jobs:
                if j[0] == "load":
                    ensure(li + 3)
                    ld = issued.pop(id(j))
                    flush_pending()
                    j[2](ld)
                    li += 1
                elif j[0] == "prefetch":
                    ensure(li + j[1])
                else:
                    flush_pending()
                    j[1]()
            flush_pending()

        def mjob(j0):
            return ("load", lambda: mjob_issue(j0), lambda v: mjob_run(j0, v))

        def mod(i):
            return modT[:, i * 8:(i + 1) * 8]

        def derive_a():
            stt(der[:, GM1:GM1 + 8], mod(1), 1.0, vB[:, N1:N1 + 8], ALU.add, ALU.mult, [t_modi[1], t_vB], [t_derA])

        def derive_b():
            ts("dve", der[:, G1:G1 + 8], mod(2), 0.5, None, ALU.mult, None, [t_modi[2]], [t_derB])

        def derive_c():
            rd = [t_modi[4], t_vB, t_flg]
            stt(der[:, GMM:GMM + 8], mod(4), 1.0, vB[:, NM:NM + 8], ALU.add, ALU.mult, rd, [t_derC])
            ts("dve", der[:, NBD:NBD + 4], vB[:, BD:BD + 4], -1.0, None, ALU.mult, None, rd, [t_derC])
            ts("dve", der[:, W0F:W0F + 4], vB[:, CW:CW + 4], flg[:, 1:2], None, ALU.mult, None, rd, [t_derC])
            ts("dve", der[:, W2F:W2F + 4], vB[:, CW + 8:CW + 12], flg[:, 1:2], None, ALU.mult, None, rd, [t_derC])

        def derive_d():
            rd = [t_modi[7], t_modi[8], t_vB]
            stt(der[:, GM2:GM2 + 8], mod(7), 1.0, vB[:, N2:N2 + 8], ALU.add, ALU.mult, rd, [t_derD])
            ts("dve", der[:, G2:G2 + 8], mod(8), 0.5, None, ALU.mult, None, rd, [t_derD])

        def norm_stats(blk, sl=0):
            for kc in range(8):
                s = kc % 2
                act(sqr[:, s, :], xT[:, kc, blk * 512:(blk + 1) * 512], AF.Square, [t_xT[kc][blk]], [t_sq[s]])
                mm(ps[:, 7, :], onesB[:], sqr[:, s, :], kc == 0, kc == 7, [t_ones, t_sq[s]], [bank(7)])
            act(lnv2[:, sl, :], ps[:, 7, :], AF.Ln, [bank(7)], [t_lnv2[sl]], scale=1.0 / D, bias=EPS)
            act(rstd2[:, sl, :], lnv2[:, sl, :], AF.Exp, [t_lnv2[sl]], [t_rstd2[sl]], scale=-0.5)

        def norm_apply(blk, gm_col, sh_mod, t_gm, act_heavy=False, tmps=None, kcs=None):
            sl = blk % 2
            if tmps is None:
                tmps = [(tmpr[:, 0, :], t_tmp[0]), (tmpr[:, 1, :], t_tmp[1])]
            for kc in (range(8) if kcs is None else kcs):
                tv, ttile = tmps[kc % len(tmps)]
                xv = xT[:, kc, blk * 512:(blk + 1) * 512]
                hv = hT[:, kc, blk * 512:(blk + 1) * 512]
                tt("dve", tv, xv, rstd2[:, sl, :], ALU.mult, [t_xT[kc][blk], t_rstd2[sl]], [ttile])
                gm = der[:, gm_col + kc:gm_col + kc + 1]
                sh = modT[:, sh_mod * 8 + kc:sh_mod * 8 + kc + 1]
                rd = [ttile, t_gm, t_modi[sh_mod]]
                if kc % 2 == 1:
                    ts("pool", hv, tv, gm, sh, ALU.mult, ALU.add, rd, [t_hT[kc][blk]])
                elif kc % 4 == 0 or act_heavy:
                    act(hv, tv, AF.Identity, rd, [t_hT[kc][blk]], scale=gm, bias=sh)
                else:
                    ts("dve", hv, tv, gm, sh, ALU.mult, ALU.add, rd, [t_hT[kc][blk]])

        def norm_mod(gm_col, sh_mod, t_gm, tmps=None):
            P.phase = "norm%d" % sh_mod
            norm_stats(0, 0)
            for blk in range(NB):
                if blk + 1 < NB:
                    norm_stats(blk + 1, (blk + 1) % 2)
                norm_apply(blk, gm_col, sh_mod, t_gm, tmps=tmps)

        def tmps4():
            sil_v = [av_f32(OFF_SIL + i * 2048, 512) for i in range(2)]
            return [(tmpr[:, 0, :], t_tmp[0]), (sil_v[0], t_sil[0]), (tmpr[:, 1, :], t_tmp[1]), (sil_v[1], t_sil[1])]

        def make_norm_tail(apply_fn, label, begin_extra=None, sq_off=None, alias_fn=None, lag=1, slice_fn=None, post_fn=None, sq_offs=None, defer=None):
            st = {}

            def begin():
                P.phase = "tail_" + label
                if begin_extra is not None:
                    begin_extra()
                t_sq8 = P.tiles("sq8", 2, 8)
                if alias_fn is not None:
                    alias_fn(t_sq8)
                else:
                    P.inherit(flat(t_sq8), P.retire(flat(t_w13)))
                    arena_tiles.extend(flat(t_sq8))
                so = OFF_W13 if sq_off is None else sq_off
                bases = (so, so + 8192) if sq_offs is None else sq_offs
                st["t"] = t_sq8
                st["v"] = [[av_bf(bases[s_] + kc * 1024, 512) for kc in range(8)] for s_ in range(2)]

            def sq_blk(b):
                s_ = b % 2
                for kc in range(8):
                    act(st["v"][s_][kc], xT[:, kc, b * 512:(b + 1) * 512], AF.Square, [t_xT[kc][b]], [st["t"][s_][kc]])

            def smm_blk(b):
                s_ = b % 2
                for kc in range(8):
                    mm(ps[:, 7, :], onesB[:], st["v"][s_][kc], kc == 0, kc == 7, [t_ones, st["t"][s_][kc]], [bank(7)])
                act(lnv2[:, s_, :], ps[:, 7, :], AF.Ln, [bank(7)], [t_lnv2[s_]], scale=1.0 / D, bias=EPS)
                act(rstd2[:, s_, :], lnv2[:, s_, :], AF.Exp, [t_lnv2[s_]], [t_rstd2[s_]], scale=-0.5)

            def hook(b):
                if b >= 1:
                    smm_blk(b - 1)
                sq_blk(b)
                if slice_fn is None and b >= lag:
                    apply_fn(b - lag)
                if slice_fn is not None and post_fn is not None and b >= 2:
                    post_fn(b - 2)

            def finish():
                smm_blk(NB - 1)
                for b in range(NB - (lag if slice_fn is None else 2), NB):
                    apply_fn(b)

            def inter(b):
                if slice_fn is None or b < 2:
                    return []
                return [(lambda kc=kc: slice_fn(b - 2, kc)) for kc in range(8)]

            return {"begin": begin, "hook": hook, "finish": finish, "inter": inter, "defer": defer}

        ffn_state = {"n13": 0, "nab": 0, "ndn": 0}

        def ffn(l, gate_col, t_gate, extra=None, pre=None, gate_tiles=(), tail=None):
            extra = extra or {}
            w1v = w1_d[l].rearrange("(k p) f -> p k f", p=128)
            w3v = w3_d[l].rearrange("(k p) f -> p k f", p=128)
            gT = [av_bf(OFF_GT + i * 4096, 2048) for i in range(8)]
            w2s = [av_bf(OFF_W2 + i * 2048, 1024) for i in range(8)]
            sil = [av_f32(OFF_SIL + i * 2048, 512) for i in range(2)]

            def p_issue(pf):
                sl1, sl3 = ring_alloc2()
                wv1 = wslot_view(OFF_W13, sl1)
                wv3 = wslot_view(OFF_W13, sl3)
                gt = list(gate_tiles) if pf < 4 else []
                dma("pool", wv1, w1v[:, :, pf * 128:(pf + 2) * 128], gt, [t_w13[sl1]], "ws%d" % sl1)
                dma("pool", wv3, w3v[:, :, pf * 128:(pf + 2) * 128], gt, [t_w13[sl3]], "ws%d" % sl3)
                return (wv1, wv3, sl1, sl3)

            def p_run(pf, f0, ld):
                P.phase = "ffn%d" % l
                wv1, wv3, sl1, sl3 = ld
                for fc in range(2):
                    fi = pf + fc - f0
                    for blk in range(NB):
                        k = ffn_state["nab"]
                        ffn_state["nab"] += 1
                        ba, bb = k % 2, 2 + k % 2
                        for kc in range(8):
                            mm(ps[:, ba, :], wv1[:, kc, fc * 128:(fc + 1) * 128], hT[:, kc, blk * 512:(blk + 1) * 512],
                               kc == 0, kc == 7, [t_w13[sl1], t_hT[kc][blk]], [bank(ba)])
                        for kc in range(8):
                            mm(ps[:, bb, :], wv3[:, kc, fc * 128:(fc + 1) * 128], hT[:, kc, blk * 512:(blk + 1) * 512],
                               kc == 0, kc == 7, [t_w13[sl3], t_hT[kc][blk]], [bank(bb)])
                        act(sil[k % 2], ps[:, ba, :], AF.Silu, [bank(ba)], [t_sil[k % 2]])
                        tt("dve", gT[fi][:, blk * 512:(blk + 1) * 512], sil[k % 2], ps[:, bb, :], ALU.mult,
                           [t_sil[k % 2], bank(bb)], [t_gT[fi][blk]])

            def w2_loads(f0, ng):
                for fi in range(ng):
                    f = f0 + fi
                    dma("pool", w2s[fi], w2_d[l][f * 128:(f + 1) * 128, :], [], [t_w2[fi]], "w2s%d" % fi)

            def down(f0, ng):
                for d in range(8):
                    for blk in range(NB):
                        k = ffn_state["ndn"]
                        ffn_state["ndn"] += 1
                        bd = 4 + k % 2
                        for fi in range(ng):
                            mm(ps[:, bd, :], w2s[fi][:, d * 128:(d + 1) * 128], gT[fi][:, blk * 512:(blk + 1) * 512],
                               fi == 0, fi == ng - 1, [t_w2[fi], t_gT[fi][blk]], [bank(bd)])
                        xv = xT[:, d, blk * 512:(blk + 1) * 512]
                        stt(xv, ps[:, bd, :], der[:, gate_col + d:gate_col + d + 1], xv, ALU.mult, ALU.add,
                            [bank(bd), t_gate, t_xT[d][blk]], [t_xT[d][blk]])

            P.phase = "ffn%d" % l
            def down_blk(f0, ng, blk, inter=()):
                inter = list(inter)
                for d in range(8):
                    if d >= 1 and inter:
                        inter.pop(0)()
                    k = ffn_state["ndn"]
                    ffn_state["ndn"] += 1
                    bd = 4 + k % 2
                    for fi in range(ng):
                        mm(ps[:, bd, :], w2s[fi][:, d * 128:(d + 1) * 128], gT[fi][:, blk * 512:(blk + 1) * 512],
                           fi == 0, fi == ng - 1, [t_w2[fi], t_gT[fi][blk]], [bank(bd)])
                    xv = xT[:, d, blk * 512:(blk + 1) * 512]
                    stt(xv, ps[:, bd, :], der[:, gate_col + d:gate_col + d + 1], xv, ALU.mult, ALU.add,
                        [bank(bd), t_gate, t_xT[d][blk]], [t_xT[d][blk]])
                while inter:
                    inter.pop(0)()

            def run_tail(f0, ng):
                tail["begin"]()
                for blk in range(NB):
                    down_blk(f0, ng, blk, tail["inter"](blk))
                    tail["hook"](blk)
                if tail.get("defer"):
                    late[tail["defer"]] = tail["finish"]
                else:
                    tail["finish"]()

            jobs = []
            pi = 0
            for gi_, (f0, f1) in enumerate(GROUPS):
                ng = f1 - f0
                if f0 != 0:
                    jobs.append(("run", (lambda f0=f0, ng=ng: w2_loads(f0, ng))))
                if f0 == 0 and pre is not None:
                    jobs.append(("prefetch", 2))
                    jobs.append(("run", pre))
                for pf in range(f0, f1, 2):
                    jobs.append(("load", (lambda pf=pf: p_issue(pf)), (lambda ld, pf=pf, f0=f0: p_run(pf, f0, ld))))
                    if pf == 0:
                        jobs.append(("run", (lambda f0=f0, ng=ng: w2_loads(f0, ng))))
                    jobs.extend(extra.get(pi, []))
                    pi += 1
                if tail is not None and gi_ == len(GROUPS) - 1:
                    jobs.append(("run", (lambda f0=f0, ng=ng: run_tail(f0, ng))))
                else:
                    jobs.append(("run", (lambda f0=f0, ng=ng: down(f0, ng))))
            run_jobs(jobs)

        late = {}

        def mixer(pre=None):
            winv = win_d.rearrange("(k p) f -> p k f", p=128)
            woutv = wout_d.rearrange("(k p) f -> p k f", p=128)
            t_lr = P.tiles("lr", NB)
            t_rm = P.tiles("rm")
            t_qk = P.tiles("qk", 4, NB)
            t_mix = P.tiles("mix", 4, NB)
            t_ws = P.tiles("ws", 3)
            t_L = P.tiles("L", 2, NB)
            t_C = P.tiles("C", 2, NB)
            t_ex = P.tiles("ex", 3)
            t_un = [t_L, t_C, t_ex]
            if pre is not None:
                P.inherit(flat(t_ws), P.retire(arena_tiles))
            else:
                remap([t_lr, t_rm, t_qk, t_mix, t_ws, t_un])
            union_tiles = flat(t_un)

            def remap_union(new):
                tok = P.retire(union_tiles)
                P.inherit(flat(new), tok)
                for t in union_tiles:
                    arena_tiles.remove(t)
                union_tiles.clear()
                union_tiles.extend(flat(new))
                arena_tiles.extend(flat(new))

            lrT = av_bf(OFF_LR, 2048)
            rmask = av_bf(OFF_RM, 2048)
            qk = [av_bf(OFF_QK + i * 4096, 2048) for i in range(4)]
            mixT = [av_bf(OFF_MIX + i * 4096, 2048) for i in range(4)]
            wsl = [wslot_view(OFF_WS, i) for i in range(3)]
            wsl2 = [av_bf(OFF_WS + i * 4096, 2048).rearrange("p (k f) -> p k f", f=512) for i in range(3)]
            wstate = {"n": 0, "issued": 0}
            wplan = []
            if not debug:
                wplan.append(("in", [(1536, 32, 0)]))
                for p_ in range(2):
                    wplan.append(("in", [(p_ * 128, 128, 0), (256 + p_ * 128, 128, 128)]))
                    wplan.append(("in", [(512 + p_ * 256, 256, 0)]))
                    wplan.append(("in", [(1024 + p_ * 256, 256, 0)]))
                    for s_ in range(20 + 8 * p_, 28 + 8 * p_):
                        wplan.append(("ada", s_))
                for dq_ in range(4):
                    wplan.append(("out", 0, dq_))
                for j_ in range(4):
                    wplan.append(("in", [(1568 + j_ * 128, 128, 0), (2080 + j_ * 128, 128, 128)]))
                    wplan.append(("in", [(2592 + j_ * 128, 128, 0)]))
                for half_ in range(2):
                    wplan.append(("out2", 1, half_))

            def w_issue(i, entry):
                s_ = i % 3
                if entry[0] == "ada":
                    dma("pool", wsl[s_], wada_v[:, :, entry[1] * 256:(entry[1] + 1) * 256], [], [t_ws[s_]], "ws%d" % s_)
                elif entry[0] == "in":
                    for (c0, ncl, do) in entry[1]:
                        dma("pool", wsl[s_][:, :, do:do + ncl], winv[:, :, c0:c0 + ncl], [], [t_ws[s_]], "ws%d" % s_)
                elif entry[0] == "out2":
                    _, part, half = entry
                    dma("pool", wsl2[s_], woutv[:, part * 4:(part + 1) * 4, half * 512:(half + 1) * 512], [],
                        [t_ws[s_]], "ws%d" % s_)
                else:
                    _, part, dq = entry
                    dma("pool", wsl[s_][:, 0:4, :], woutv[:, part * 4:(part + 1) * 4, dq * 256:(dq + 1) * 256], [],
                        [t_ws[s_]], "ws%d" % s_)

            def wnext(entry, hold=0):
                i = wstate["n"]
                wstate["n"] += 1
                if wplan:
                    assert wplan[i] == entry, (i, wplan[i], entry)
                    while wstate["issued"] < min(i + 3 - hold, len(wplan)):
                        w_issue(wstate["issued"], wplan[wstate["issued"]])
                        wstate["issued"] += 1
                else:
                    w_issue(i, entry)
                return (wsl2 if entry[0] == "out2" else wsl)[i % 3], t_ws[i % 3]

            def wload(cols, hold=0):
                return wnext(("in", cols), hold)

            def m1job(s_):
                wv_, wt_ = wnext(("ada", s_))
                flush_pending()
                for kc in range(8):
                    mm(ps[0:1, 6, 0:256], scB[:, kc:kc + 1], wv_[:, kc, :], kc == 0, kc == 7, [wt_, t_sc], [bank(6)])
                act(stA[0:1, :], ps[0:1, 6, 0:128], AF.Copy, [bank(6)], [t_stA])
                act(stB[0:1, :], ps[0:1, 6, 128:256], AF.Copy, [bank(6)], [t_stB])
                j0 = 2 * s_

                def k1():
                    mm(ps[:, 7, j0:j0 + 1], stA[0:1, :], identF[0:1, 0:1], True, True, [t_stA, t_cst], [bank(7)])
                    mm(ps[:, 7, j0 + 1:j0 + 2], stB[0:1, :], identF[0:1, 0:1], True, True, [t_stB, t_cst], [bank(7)])
                    tt("dve", modT[:, j0:j0 + 2], ps[:, 7, j0:j0 + 2], vA[:, j0:j0 + 2], ALU.add, [bank(7), t_vA],
                       [t_modi[j0 // 8]])
                pending.append(k1)

            def blkv(ap, blk):
                return ap[:, blk * 512:(blk + 1) * 512]

            while wplan and wstate["issued"] < 3:
                w_issue(wstate["issued"], wplan[wstate["issued"]])
                wstate["issued"] += 1
            if pre is not None:
                pre()
                remap([t_lr, t_rm, t_qk, t_mix, t_un])
                arena_tiles.extend(flat(t_ws))
            memset("pool", rmask, 1.0, [t_rm])
            memset("pool", rmask[:, 0:T:128], 0.0, [t_rm])

            P.phase = "mix_lr"
            wv, wt = wload([(1536, 32, 0)])
            for blk in range(NB):
                b = blk % 2
                for kc in range(8):
                    mm(ps[0:32, b, :], wv[:, kc, 0:32], blkv(hT[:, kc, :], blk), kc == 0, kc == 7,
                       [wt, t_hT[kc][blk]], [bank(b)])
                cp("act", blkv(lrT[0:32, :], blk), ps[0:32, b, :], [bank(b)], [t_lr[blk]])

            if debug == "m0":
                raise StopBuild()
            for p in (range(2) if debug != "conv" else []):
                P.phase = "prep%d" % p
                if debug == "m4" and p == 1:
                    raise StopBuild()
                if p == 1:
                    t_L2 = P.tiles("L", 2, NB)
                    t_C2 = P.tiles("C", 2, NB)
                    t_ex2 = P.tiles("ex", 3)
                    remap_union([t_L2, t_C2, t_ex2])
                    t_Lp, t_Cp, t_exp = t_L2, t_C2, t_ex2
                else:
                    t_Lp, t_Cp, t_exp = t_L, t_C, t_ex
                Lb = [av_f32(OFF_L + i * 8192, 2048) for i in range(2)]
                Cb = [av_f32(OFF_C + i * 8192, 2048) for i in range(2)]
                exb = [av_f32(OFF_EX + i * 2048, 512) for i in range(3)]
                nex = [0]

                def exslot():
                    i = nex[0] % 3
                    nex[0] += 1
                    return exb[i], t_exp[i]

                for blk in range(NB):
                    evs = []
                    for dr in range(2):
                        b = 2 + dr
                        mm(ps[:, b, :], Wd[0:32, dr * 2 + p, :], blkv(lrT[0:32, :], blk), True, True,
                           [t_Wd, t_lr[blk]], [bank(b)])
                        evs.append(exslot())
                    for dr in range(2):
                        ev, et = evs[dr]
                        act(ev, ps[:, 2 + dr, :], AF.Exp, [bank(2 + dr), t_derC], [et], scale=-1.0,
                            bias=der[:, NBD + dr * 2 + p:NBD + dr * 2 + p + 1])
                    for dr in range(2):
                        ev, et = evs[dr]
                        act(blkv(Lb[dr], blk), ev, AF.Ln, [et], [t_Lp[dr][blk]], bias=1.0)
                    for dr in range(2):
                        P.add("dve", (lambda dr=dr, blk=blk: lambda e: e.tensor_tensor_scan(
                            out=blkv(Cb[dr], blk), data0=blkv(rmask, blk), data1=blkv(Lb[dr], blk), initial=0.0,
                            op0=ALU.mult, op1=ALU.add))(),
                            reads=flat([t_rm, t_Lp[dr][blk]]), writes=flat([t_Cp[dr][blk]]))
                    c3 = blkv(Cb[0], blk).rearrange("p (c t) -> p c t", t=128)
                    tt("dve", blkv(Lb[0], blk).rearrange("p (c t) -> p c t", t=128),
                       c3[:, :, 127:128].to_broadcast([128, 4, 128]), c3,
                       ALU.subtract, [t_Cp[0][blk]], [t_Lp[0][blk]])
                    tt("dve", blkv(Lb[1], blk), blkv(Cb[1], blk), blkv(Lb[1], blk), ALU.subtract,
                       [t_Cp[1][blk], t_Lp[1][blk]], [t_Lp[1][blk]])
                for dr in range(2):
                    act(eaT[:, dr, :], Cb[dr][:, 127:T:128], AF.Exp, [t_Cp[dr]], [t_ea], scale=-1.0 / 16)
                ts("dve", eaT[:, 0, 0:16:2], eaT[:, 0, 0:16:2], flg[:, 0:1], None, ALU.mult, None, [t_ea, t_flg], [t_ea])
                ts("dve", eaT[:, 1, 1:16:2], eaT[:, 1, 1:16:2], flg[:, 0:1], None, ALU.mult, None, [t_ea, t_flg], [t_ea])

                if debug == "m1":
                    raise StopBuild()
                P.phase = "qkvg%d" % p
                wv, wt = wload([(p * 128, 128, 0), (256 + p * 128, 128, 128)])
                for blk in range(NB):
                    bq, bk = blk % 2, 2 + blk % 2
                    for kc in range(8):
                        mm(ps[:, bq, :], wv[:, kc, 0:128], blkv(hT[:, kc, :], blk), kc == 0, kc == 7,
                           [wt, t_hT[kc][blk]], [bank(bq)])
                    for kc in range(8):
                        mm(ps[:, bk, :], wv[:, kc, 128:256], blkv(hT[:, kc, :], blk), kc == 0, kc == 7,
                           [wt, t_hT[kc][blk]], [bank(bk)])
                    for dr in range(2):
                        sgn = 1.0 / 16
                        eq, etq = exslot()
                        act(eq, blkv(Lb[dr], blk), AF.Exp, [t_Lp[dr][blk]], [etq], scale=sgn)
                        stt(blkv(qk[2 * dr], blk), ps[:, bq, :], 0.125, eq, ALU.mult, ALU.mult,
                            [bank(bq), etq], [t_qk[2 * dr][blk]])
                        ek, etk = exslot()
                        act(ek, blkv(Lb[dr], blk), AF.Exp, [t_Lp[dr][blk]], [etk], scale=-sgn)
                        tt("dve", blkv(qk[2 * dr + 1], blk), ps[:, bk, :], ek, ALU.mult,
                           [bank(bk), etk], [t_qk[2 * dr + 1][blk]])

                t_kt = P.tiles("kt", NB)
                t_vt = P.tiles("vt", 16)
                t_sg = P.tiles("sg", 2, NB)
                t_xp = P.tiles("xp", 2, 16)
                t_at = P.tiles("at", 2, 2)
                t_X = P.tiles("X", 2, 4)
                remap_union([t_kt, t_vt, t_sg, t_xp, t_at, t_X])
                ktok = av_bf(OFF_KT, 4096).rearrange("p (c f) -> p c f", f=256)
                vtok = av_bf(OFF_VT, 4096).rearrange("p (c f) -> p c f", f=256)
                sg = [av_bf(OFF_SG + i * 4096, 2048) for i in range(2)]
                xpb = av_bf(OFF_XP, 4096).rearrange("p (r c e) -> p r c e", c=16, e=128)
                atb = av_bf(OFF_AT, 1024).rearrange("p (h s f) -> p h s f", s=2, f=256)
                Xr = av_f32(OFF_X, 1024).rearrange("p (r s e) -> p r s e", s=4, e=128)

                for blk in range(NB):
                    b = 4 + blk % 2
                    pb = ps[:, b, :].bitcast(BF16).rearrange("p (c f) -> p c f", f=256)
                    for cc in range(4):
                        c = blk * 4 + cc
                        for dr in range(2):
                            tr(pb[:, cc, dr * 128:(dr + 1) * 128], qk[2 * dr + 1][:, c * 128:(c + 1) * 128], identB[:],
                               [t_qk[2 * dr + 1][blk], t_idB], [bank(b)])
                    cp(alt_eng(), ktok[:, blk * 4:(blk + 1) * 4, :], pb, [bank(b)], [t_kt[blk]])

                wv, wt = wload([(512 + p * 256, 256, 0)])
                for c in range(16):
                    b = c % 4
                    for kc in range(8):
                        mm(ps[:, b, 0:256], hT[:, kc, c * 128:(c + 1) * 128], wv[:, kc, :], kc == 0, kc == 7,
                           [wt, t_hT[kc][c // 4]], [bank(b)])
                    cp(alt_eng(), vtok[:, c, :], ps[:, b, 0:256], [bank(b)], [t_vt[c]])

                wvg, wtg = wload([(1024 + p * 256, 256, 0)])

                def gproj(i):
                    hh, blk = i // NB, i % NB
                    b = i % 4
                    for kc in range(8):
                        mm(ps[:, b, :], wvg[:, kc, hh * 128:(hh + 1) * 128], blkv(hT[:, kc, :], blk), kc == 0, kc == 7,
                           [wtg, t_hT[kc][blk]], [bank(b)])
                    act(blkv(sg[hh], blk), ps[:, b, :], AF.Silu, [bank(b)], [t_sg[hh][blk]])

                if debug == "m2":
                    raise StopBuild()
                P.phase = "chain%d" % p
                batches = []
                orders = {1: list(range(15, -1, -1)), 0: list(range(16))}
                for i in range(0, 16, 4):
                    for dr in (1, 0):
                        batches.append((dr, orders[dr][i:i + 4]))
                chst = {1: [s0sb[:, 2 + p, :], t_s0, 0], 0: [s0sb[:, p, :], t_s0, 0]}

                PB = (4, 5, 6, 7)

                def pmm(bi):
                    dr, cs = batches[bi]
                    b = PB[bi % 4]
                    for qq, c in enumerate(cs):
                        for hh in range(2):
                            mm(ps[hh * 64:(hh + 1) * 64, b, qq * 128:(qq + 1) * 128],
                               ktok[:, c, dr * 128 + hh * 64:dr * 128 + (hh + 1) * 64], vtok[:, c, hh * 128:(hh + 1) * 128],
                               True, True, [t_kt[c // 4], t_vt[c]], [bank(b)])

                def chain_step(bi, qq):
                    dr, cs = batches[bi]
                    b = PB[bi % 4]
                    c = cs[qq]
                    if True:
                        cur_ap, cur_t, step = chst[dr]
                        ea = eaT[:, dr, c:c + 1]
                        act(xpb[:, dr, c, :], cur_ap, AF.Identity, [cur_t, t_ea], [t_xp[dr][c]], scale=ea)
                        ns = step % 4
                        stt(Xr[:, dr, ns, :], cur_ap, ea, ps[:, b, qq * 128:(qq + 1) * 128], ALU.mult, ALU.add,
                            [cur_t, t_ea, bank(b)], [t_X[dr][ns]])
                        chst[dr] = [Xr[:, dr, ns, :], t_X[dr][ns], step + 1]
                        seg_end = (c % 2 == 1) if dr == 0 else (c % 2 == 0)
                        if seg_end:
                            dma("sp", st_d[c // 2, dr, 2 * p:2 * p + 2].rearrange("h d e -> (h d) e"), Xr[:, dr, ns, :],
                                [t_X[dr][ns]], [], "stx%d_%d" % (dr, ns))

                for bi in range(4):
                    pmm(bi)
                for k in range(4):
                    for qq in range(4):
                        chain_step(2 * k, qq)
                        chain_step(2 * k + 1, qq)
                    gproj(2 * k)
                    gproj(2 * k + 1)
                    if 2 * k + 4 < len(batches):
                        pmm(2 * k + 4)
                        pmm(2 * k + 5)

                if debug == "m3":
                    raise StopBuild()
                P.phase = "core%d" % p
                SB = (0, 1, 4, 5)

                def sbank(c, hh):
                    return SB[(c % 2) * 2 + hh]

                def scores(c, hh):
                    if c >= 16:
                        return
                    for dr in range(2):
                        sbk = sbank(c, hh)
                        mm(ps[:, sbk, dr * 128:(dr + 1) * 128],
                           qk[2 * dr + 1][hh * 64:(hh + 1) * 64, c * 128:(c + 1) * 128],
                           qk[2 * dr][hh * 64:(hh + 1) * 64, c * 128:(c + 1) * 128], True, True,
                           [t_qk[2 * dr + 1][c // 4], t_qk[2 * dr][c // 4]], [bank(sbk)])

                def maskpv(c):
                    blk = c // 4
                    scores(c + 1, 0)
                    scores(c + 1, 1)
                    for hh in range(2):
                        sbk = sbank(c, hh)
                        tt("dve", atb[:, hh, c % 2, :], ps[:, sbk, 0:256], maskFB, ALU.mult,
                           [bank(sbk), t_cst], [t_at[hh][c % 2]])
                    for hh in range(2):
                        ob = 2 + hh
                        ov = ps[:, ob, (c % 4) * 128:(c % 4 + 1) * 128]
                        vv = vtok[:, c, hh * 128:(hh + 1) * 128]
                        mm(ov, vv, atb[:, hh, c % 2, 0:128], True, False, [t_vt[c], t_at[hh][c % 2]], [bank(ob)])
                        mm(ov, vv, atb[:, hh, c % 2, 128:256], False, False, [t_vt[c], t_at[hh][c % 2]], [bank(ob)])
                        for dr in range(2):
                            mm(ov, xpb[hh * 64:(hh + 1) * 64, dr, c, :], qk[2 * dr][hh * 64:(hh + 1) * 64, c * 128:(c + 1) * 128],
                               False, dr == 1, [t_xp[dr][c], t_qk[2 * dr][blk]], [bank(ob)])
                    if c % 4 == 3:
                        for hh in range(2):
                            ob = 2 + hh
                            act(tmpr[:, hh, :], ps[:, ob, :], AF.Copy, [bank(ob)], [t_tmp[hh]])
                            act(sqr[:, hh, :], tmpr[:, hh, :], AF.Square, [t_tmp[hh]], [t_sq[hh]])
                        deferred.append((c + 2, blk))

                def onorm_tail(blk):
                    for hh in range(2):
                        sbk = 7 - hh
                        mm(ps[:, sbk, :], onesB[:], sqr[:, hh, :], True, True, [t_ones, t_sq[hh]], [bank(sbk)])
                    for hh in range(2):
                        sbk = 7 - hh
                        act(lnv2[:, hh, :], ps[:, sbk, :], AF.Ln, [bank(sbk)], [t_lnv2[hh]], scale=1.0 / 128, bias=EPS)
                    for hh in range(2):
                        act(rstd2[:, hh, :], lnv2[:, hh, :], AF.Exp, [t_lnv2[hh]], [t_rstd2[hh]], scale=-0.5)
                    for hh in range(2):
                        tt("dve", tmpr[:, hh, :], tmpr[:, hh, :], rstd2[:, hh, :], ALU.mult, [t_tmp[hh], t_rstd2[hh]], [t_tmp[hh]])
                    for hh in range(2):
                        hg = p * 2 + hh
                        stt(blkv(mixT[hg], blk), tmpr[:, hh, :], vB[:, GLN + hg:GLN + hg + 1], blkv(sg[hh], blk),
                            ALU.mult, ALU.mult, [t_tmp[hh], t_vB, t_sg[hh][blk]], [t_mix[hg][blk]])

                deferred = []
                scores(0, 0)
                scores(0, 1)
                for c in range(16):
                    maskpv(c)
                    while deferred and deferred[0][0] <= c:
                        onorm_tail(deferred.pop(0)[1])
                    if c % 2 == 1 and not debug:
                        m1job(20 + 8 * p + c // 2)
                while deferred:
                    onorm_tail(deferred.pop(0)[1])
                flush_pending()
                if p == 1 and not debug:
                    derive_d()

            def wout_part(part, t_mixp):
                P.phase = "wout%d" % part
                for dq in range(4):
                    wv_, wt_ = wnext(("out", part, dq))
                    for dd in range(2):
                        d = dq * 2 + dd
                        for blk in range(NB):
                            k = ffn_state["ndn"]
                            ffn_state["ndn"] += 1
                            bd = 4 + k % 4
                            for kk in range(4):
                                mm(ps[:, bd, :], wv_[:, kk, dd * 128:(dd + 1) * 128], blkv(mixT[kk], blk), kk == 0, kk == 3,
                                   [wt_, t_mixp[kk][blk]], [bank(bd)])
                            xv = xT[:, d, blk * 512:(blk + 1) * 512]
                            stt(xv, ps[:, bd, :], modT[:, 40 + d:41 + d], xv, ALU.mult, ALU.add,
                                [bank(bd), t_modi[5], t_xT[d][blk]], [t_xT[d][blk]])

            def wout_part_tail(part, t_mixp, tail):
                P.phase = "wout%d" % part
                wvA, wtA = wnext(("out2", part, 0))
                wvB, wtB = wnext(("out2", part, 1), hold=1)
                tail["begin"]()
                for blk in range(NB):
                    fill = list(tail["inter"](blk))
                    for d in range(8):
                        if d >= 1 and fill:
                            fill.pop(0)()
                        wv_, wt_ = (wvA, wtA) if d < 4 else (wvB, wtB)
                        dd = d % 4
                        k = ffn_state["ndn"]
                        ffn_state["ndn"] += 1
                        bd = 4 + k % 2
                        for kk in range(4):
                            mm(ps[:, bd, :], wv_[:, kk, dd * 128:(dd + 1) * 128], blkv(mixT[kk], blk), kk == 0, kk == 3,
                               [wt_, t_mixp[kk][blk]], [bank(bd)])
                        xv = xT[:, d, blk * 512:(blk + 1) * 512]
                        stt(xv, ps[:, bd, :], modT[:, 40 + d:41 + d], xv, ALU.mult, ALU.add,
                            [bank(bd), t_modi[5], t_xT[d][blk]], [t_xT[d][blk]])
                    while fill:
                        fill.pop(0)()
                    tail["hook"](blk)
                late["finish6"] = tail["finish"]

            if debug == "gla":
                for i in range(4):
                    for blk in range(NB):
                        cp("dve", xT[:, i, blk * 512:(blk + 1) * 512], blkv(mixT[i], blk), [t_mix[i][blk]], [t_xT[i][blk]])
                        cp("dve", xT[:, 4 + i, blk * 512:(blk + 1) * 512], blkv(qk[i], blk), [t_qk[i][blk]], [t_xT[4 + i][blk]])
                return
            if debug != "conv":
                wout_part(0, t_mix)

            P.phase = "conv"
            t_cv = P.tiles("cv", 3)
            remap_union([t_cv])
            cvb = [av_f32(OFF_CV + i * 2048, 512) for i in range(3)]
            for j in range(4):
                wvA, wtA = wload([(1568 + j * 128, 128, 0), (2080 + j * 128, 128, 128)])
                wvB, wtB = wload([(2592 + j * 128, 128, 0)], hold=1)
                for blk in range(NB):
                    b0 = 0 if (j * NB + blk) % 2 == 0 else 3
                    bcb, bcc, bch = b0, b0 + 1, b0 + 2
                    for (bb, wvx, wtx, co) in ((bcb, wvA, wtA, 0), (bcc, wvA, wtA, 128), (bch, wvB, wtB, 0)):
                        for kc in range(8):
                            mm(ps[:, bb, :], wvx[:, kc, co:co + 128], blkv(hT[:, kc, :], blk), kc == 0, kc == 7,
                               [wtx, t_hT[kc][blk]], [bank(bb)])
                    ccs, u, cu = cvb
                    act(ccs, ps[:, bcc, :], AF.Copy, [bank(bcc)], [t_cv[0]])
                    tt("dve", u, ccs, ps[:, bch, :], ALU.mult, [t_cv[0], bank(bch)], [t_cv[1]])
                    act(cu, u, AF.Identity, [t_cv[1], t_vB], [t_cv[2]], scale=vB[:, CW + 4 + j:CW + 5 + j])
                    u3 = u.rearrange("p (r t) -> p r t", t=64)
                    cu3 = cu.rearrange("p (r t) -> p r t", t=64)
                    stt(cu3[:, :, 1:64], u3[:, :, 0:63], vB[:, CW + j:CW + j + 1], cu3[:, :, 1:64], ALU.mult, ALU.add,
                        [t_cv[1], t_cv[2], t_vB], [t_cv[2]])
                    stt(cu3[:, :, 0:63], u3[:, :, 1:64], vB[:, CW + 8 + j:CW + 9 + j], cu3[:, :, 0:63], ALU.mult, ALU.add,
                        [t_cv[1], t_cv[2], t_vB], [t_cv[2]])
                    u4 = u.rearrange("p (s r t) -> p s r t", r=4, t=64)
                    cu4 = cu.rearrange("p (s r t) -> p s r t", r=4, t=64)
                    stt(cu4[:, :, 1:4, 0:1], u4[:, :, 0:3, 63:64], der[:, W0F + j:W0F + j + 1], cu4[:, :, 1:4, 0:1],
                        ALU.mult, ALU.add, [t_cv[1], t_cv[2], t_derC], [t_cv[2]])
                    stt(cu4[:, :, 0:3, 63:64], u4[:, :, 1:4, 0:1], der[:, W2F + j:W2F + j + 1], cu4[:, :, 0:3, 63:64],
                        ALU.mult, ALU.add, [t_cv[1], t_cv[2], t_derC], [t_cv[2]])
                    tt("dve", blkv(mixT[j], blk), cu, ps[:, bcb, :], ALU.mult, [t_cv[2], bank(bcb), t_mix[j][blk]],
                       [t_mix[j][blk]])
            if debug == "conv":
                for i in range(4):
                    for blk in range(NB):
                        cp("dve", xT[:, 4 + i, blk * 512:(blk + 1) * 512], blkv(mixT[i], blk), [t_mix[i][blk]], [t_xT[4 + i][blk]])
                return
            if debug:
                wout_part(1, t_mix)
            else:
                t_x6 = P.tiles("x6", 2)
                x6v = [av_f32(OFF_U + 24576 + i * 2048, 512) for i in range(2)]
                tmps6 = [(tmpr[:, 0, :], t_tmp[0]), (x6v[0], t_x6[0]), (tmpr[:, 1, :], t_tmp[1]), (x6v[1], t_x6[1])]
                wout_part_tail(1, t_mix, make_norm_tail(
                    lambda b: norm_apply(b, GM2, 6, t_derD, act_heavy=True, tmps=tmps6), "norm6",
                    sq_off=OFF_U + 8192, alias_fn=lambda t: remap_union([t, t_x6]),
                    slice_fn=lambda b, kc: norm_apply(b, GM2, 6, t_derD, act_heavy=True, tmps=tmps6, kcs=[kc])))

        fin = {"n": 0}

        def final_begin():
            t_yt = P.tiles("yt", 8)
            t_ys = P.tiles("ys", 2)
            tok = P.retire(flat(t_hT))
            P.inherit(flat([t_yt, t_ys]), tok)
            hflat = hT[:].rearrange("p k t -> p (k t)")
            fin["t_yt"], fin["t_ys"] = t_yt, t_ys
            fin["yt"] = [hflat[:, i * 1024:(i + 1) * 1024].bitcast(F32) for i in range(8)]
            fin["ys"] = [hflat[:, 8192 + i * 2048:8192 + (i + 1) * 2048].bitcast(F32) for i in range(2)]

        def final_apply(blk, part="all", kcs=None):
            t_yt, t_ys, yt, ys = fin["t_yt"], fin["t_ys"], fin["yt"], fin["ys"]
            if part in ("all", "stt"):
                for kc in (range(8) if kcs is None else kcs):
                    stt(yt[kc], xT[:, kc, blk * 512:(blk + 1) * 512], vB[:, FN + kc:FN + kc + 1], rstd2[:, blk % 2, :], ALU.mult, ALU.mult,
                        [t_xT[kc][blk], t_vB, t_rstd2[blk % 2]], [t_yt[kc]])
            if part in ("all", "rest"):
                n = fin["n"]
                for j in range(4):
                    s = n % 2
                    n += 1
                    for half in range(2):
                        b = (n * 2 + half) % 4
                        for kk in range(4):
                            tr(ps[:, b, kk * 128:(kk + 1) * 128], yt[half * 4 + kk][:, j * 128:(j + 1) * 128], identF,
                               [t_yt[half * 4 + kk], t_cst], [bank(b)])
                        cp("act", ys[s][:, half * 512:(half + 1) * 512], ps[:, b, :], [bank(b)], [t_ys[s]])
                    r0 = (blk * 4 + j) * 128
                    dma("sp", y_d[r0:r0 + 128, :], ys[s], [t_ys[s]], [], "y%d" % s)
                fin["n"] = n

        remap_dummy = None
        arena_tiles.extend(flat([t_w13]))
        t_ada = P.tiles("ada", 8)
        hflat0 = hT[:].rearrange("p k t -> p (k t)")
        ada_v = [hflat0[:, i * 2048:(i + 1) * 2048].rearrange("p (k f) -> p k f", f=256) for i in range(8)]
        for i in range(8):
            dma("pool", ada_v[i], wada_v[:, :, i * 256:(i + 1) * 256], [], [t_ada[i]], "ada%d" % i)

        def m_early(j0):
            views = [(ada_v[j0 // 2 + i], None) for i in range(2)]
            tl = [t_ada[j0 // 2 + i] for i in range(2)]
            k = mods_state["bank"]
            mods_state["bank"] += 1
            rs = k % 2
            flush_pending()
            for i, (wv, _) in enumerate(views):
                for kc in range(8):
                    mm(ps[0:1, 6, i * 256:(i + 1) * 256], scB[:, kc:kc + 1], wv[:, kc, :], kc == 0, kc == 7,
                       [tl[i], t_sc], [bank(6)])
            act(lnv2[0:1, rs, :], ps[0:1, 6, :], AF.Copy, [bank(6)], [t_lnv2[rs]])

            def k1():
                for jj in range(4):
                    j = j0 + jj
                    mm(ps[:, 7, j:j + 1], lnv2[0:1, rs, jj * 128:(jj + 1) * 128], identF[0:1, 0:1], True, True,
                       [t_lnv2[rs], t_cst], [bank(7)])
                tt("dve", modT[:, j0:j0 + 4], ps[:, 7, j0:j0 + 4], vA[:, j0:j0 + 4], ALU.add, [bank(7), t_vA],
                   [t_modi[j0 // 8]])
            pending.append(k1)

        in_transposes(0)
        m_early(0)
        in_transposes(1)
        m_early(4)
        in_transposes(2)
        m_early(8)
        m_early(12)
        flush_pending()
        derive_a()
        in_transposes(3)
        P.inherit(flat(t_hT), P.retire(t_ada))
        for dr in range(2):
            for p in range(2):
                dma("pool", Wd[dr * 16:(dr + 1) * 16, dr * 2 + p, :], wdec_d[dr, :, p * 128:(p + 1) * 128],
                    [], [t_Wd4[dr * 2 + p]], "c5_%d" % (dr * 2 + p))
        P.inherit(flat(t_gT), P.retire([t_xs[0], t_xs[1]]))
        P.inherit(flat(t_w2), P.retire([t_xs[2]]))
        P.inherit(flat(t_sil), P.retire([t_xs[3]]))
        for t in flat(t_xs):
            arena_tiles.remove(t)
        arena_tiles.extend(flat([t_gT, t_w2, t_sil]))
        extra = {0: [mjob(16), mjob(20), ("run", derive_b)], 1: [mjob(24)], 2: [mjob(28)], 3: [mjob(32)],
                 4: [mjob(36), ("run", derive_c)]}
        ffn(0, G1, t_derB, extra, pre=lambda: norm_mod(GM1, 0, t_derA, tmps=tmps4()), gate_tiles=[],
            tail=(make_norm_tail(lambda b: norm_apply(b, GMM, 3, t_derC, act_heavy=True, tmps=tmps4()), "norm3",
                                 slice_fn=lambda b, kc: norm_apply(b, GMM, 3, t_derC, act_heavy=True, tmps=tmps4(), kcs=[kc]),
                                 sq_offs=(OFF_W13, 77824), defer="finish3")
                  if debug != "ffn1" else None))
        if debug != "ffn1":
            try:
                mixer(pre=(lambda: late["finish3"]()))
            except StopBuild:
                pass
            if debug not in ("mixer", "gla", "conv", "m0", "m1", "m2", "m3", "m4"):
                t_w13b = P.tiles("w13", 6)
                t_w2b = P.tiles("w2s", 8)
                t_gTb = P.tiles("gT", 8, NB)
                t_silb = P.tiles("sil", 2)
                P.inherit(flat(t_w13b), P.retire(arena_tiles))
                t_w13[:] = t_w13b

                def pre2():
                    late["finish6"]()
                    remap([t_w2b, t_gTb, t_silb])
                    arena_tiles.extend(flat(t_w13b))
                    t_w2[:] = t_w2b
                    t_gT[:] = t_gTb
                    t_sil[:] = t_silb

                ffn(1, G2, t_derD, pre=pre2,
                    tail=make_norm_tail(final_apply, "final", begin_extra=final_begin, lag=2,
                                        slice_fn=lambda b, kc: final_apply(b, "stt", [kc]),
                                        post_fn=lambda b: final_apply(b, "rest")))
        if debug:
            for kc in range(8):
                dma("sp", dbg_d[kc], xT[:, kc, :], [t_xT[kc]], [], "dbg%d" % kc)

        P.finalize()
        nc._phase_list = [op.phase for op in P.ops if op.eng == "pe"]
        for k in P.dma_counts:
            getsem(("dma", k))
        out_keys = [k for k in P.dma_counts if k.startswith("y") or k.startswith("stx") or k.startswith("dbg")]
        with nc.Block() as block:
            @block.tensor
            def _(h):
                emit_engine(P, "pe", h, sems)

            @block.scalar
            def _(h):
                emit_engine(P, "act", h, sems)

            @block.vector
            def _(h):
                emit_engine(P, "dve", h, sems)

            @block.gpsimd
            def _(h):
                emit_engine(P, "pool", h, sems)

            @block.sync
            def _(h):
                emit_engine(P, "sp", h, sems)
                for k in out_keys:
                    h.wait_ge(sems[("dma", k)], 16 * P.dma_counts[k])
    return nc


def _core_inputs(inp):
    f = lambda a: np.ascontiguousarray(np.asarray(a, dtype=np.float32))
    xp = f(inp["x_prompt"])
    xsm = f(inp["x_sample"])
    sg = f(inp["state_gla"])
    c = f(inp["c"])
    cctx = f(inp["c_ctx"])
    vecB = np.concatenate([
        f(inp["norm_ffn1"]).reshape(8, 128), f(inp["norm_mix"]).reshape(8, 128),
        f(inp["norm_ffn2"]).reshape(8, 128), f(inp["final_norm"]).reshape(8, 128),
        f(inp["gla_norm"]).reshape(4, 128), f(inp["conv_w"]).reshape(12, 128),
        f(inp["b_decay"]).reshape(4, 128)], axis=0)
    ident = np.eye(128, dtype=np.float32)
    s_i = np.arange(128)[:, None]
    t_i = np.arange(128)[None, :]
    cst = np.concatenate([ident, (s_i <= t_i).astype(np.float32), (s_i >= t_i).astype(np.float32)], axis=1)
    shared = {
        "vecB": np.ascontiguousarray(vecB), "cst": np.ascontiguousarray(cst),
        "w_ada": f(inp["w_ada"])[0],
        "w1_ffn1": f(inp["w1_ffn1"])[0], "w3_ffn1": f(inp["w3_ffn1"])[0], "w2_ffn1": f(inp["w2_ffn1"])[0],
        "w1_ffn2": f(inp["w1_ffn2"])[0], "w3_ffn2": f(inp["w3_ffn2"])[0], "w2_ffn2": f(inp["w2_ffn2"])[0],
        "w_in": f(inp["w_in"])[0], "w_decay": f(inp["w_decay"])[0], "w_out": f(inp["w_out"])[0],
    }
    b_ada = f(inp["b_ada"]).reshape(72, 128)
    maps = []
    for ci in range(8):
        m = dict(shared)
        if ci < 4:
            m["x"] = np.ascontiguousarray(xsm[ci])
            cv = c[ci]
            m["s0"] = np.ascontiguousarray(sg[ci, 0])
            fl = np.array([1.0, 0.0], np.float32)
        else:
            j = ci - 4
            m["x"] = np.ascontiguousarray(xp[8 * j:8 * j + 8].reshape(T, D))
            cv = cctx
            m["s0"] = np.zeros((2, 4, 64, 128), np.float32)
            fl = np.array([0.0, 1.0], np.float32)
        m["vecA"] = np.ascontiguousarray(np.concatenate([b_ada, cv.reshape(8, 128)], axis=0))
        m["flags"] = np.ascontiguousarray(np.broadcast_to(fl[None, :], (128, 2)))
        maps.append(m)
    return maps


_NC_CACHE = {}


def kernel(**inputs):
    maps = _core_inputs(inputs)
    if "nc" not in _NC_CACHE:
        _NC_CACHE["nc"] = build_program()
    nc = _NC_CACHE["nc"]
    res = run_bass_kernel_spmd(nc, maps, core_ids=list(range(8)))
    outs = res.results
    y_sample = np.stack([np.asarray(outs[ci]["y"], dtype=np.float32) for ci in range(4)], axis=0)
    y_prompt = np.concatenate([np.asarray(outs[ci]["y"], dtype=np.float32).reshape(8, 256, D) for ci in range(4, 8)], axis=0)
    st = np.concatenate([np.asarray(outs[ci]["st"], dtype=np.float32) for ci in range(4, 8)], axis=0)
    new_state = np.ascontiguousarray(st.reshape(32, 1, 2, 4, 64, 128))
    return (y_prompt, y_sample, new_state)
```
